# Optimizing a Trainium2 kernel written in Bass

```python
import math
import jax, jax.numpy as jnp
from jax import lax
import numpy as np

D_MODEL = 1024
BATCH = 8
SEQ = 4096
DEPTH = 2

N_MIXERS = 2
N_A = (DEPTH + 1) // 2
N_B = DEPTH // 2

LRU_WIDTH = D_MODEL
N_LRU_BLOCKS = 4
LRU_BLOCK = LRU_WIDTH // N_LRU_BLOCKS
CONV_WIDTH = 4
LRU_C = 8.0
A_MIN = 0.9
A_MAX = 0.999

HEAD_DIM = 64
N_HEADS = D_MODEL // HEAD_DIM
Q_BLOCK = 128

D_FF = int(math.ceil(8 * D_MODEL / 3 / 256)) * 256

RMS_EPS = 1e-6

kernel_name = "hybrid_rglru_stickbreaking_swiglu"


def rms_norm(x, g):
    xf = x.astype(jnp.float32)
    y = xf * lax.rsqrt(jnp.mean(xf * xf, axis=-1, keepdims=True) + RMS_EPS)
    return (y * g.astype(jnp.float32)).astype(x.dtype)


def causal_depthwise_conv(x, w, b):
    c = x.shape[-1]
    y = lax.conv_general_dilated(
        x, w.reshape(CONV_WIDTH, 1, c).astype(x.dtype),
        window_strides=(1,), padding=[(CONV_WIDTH - 1, 0)],
        dimension_numbers=("NWC", "WIO", "NWC"), feature_group_count=c)
    return y + b


def _linear_recurrence_combine(e1, e2):
    a1, b1 = e1
    a2, b2 = e2
    return a1 * a2, a2 * b1 + b2


def rg_lru_block(x, w_in, b_in, conv_w, conv_b, gate_w, gate_b, lam, w_out, b_out):
    bsz, seq, _ = x.shape
    u = jnp.einsum("bsd,de->bse", x, w_in) + b_in
    gate_branch, rec = u[..., :LRU_WIDTH], u[..., LRU_WIDTH:]
    rec = causal_depthwise_conv(rec, conv_w, conv_b)
    rec_blk = rec.reshape(bsz, seq, N_LRU_BLOCKS, LRU_BLOCK)
    gl = jnp.einsum("bsnc,gncd->gbsnd", rec_blk, gate_w).reshape(2, bsz, seq, LRU_WIDTH)
    gl = gl.astype(jnp.float32) + gate_b.astype(jnp.float32)[:, None, None, :]
    r_gate = jax.nn.sigmoid(gl[0])
    i_gate = jax.nn.sigmoid(gl[1])
    log_a = -LRU_C * r_gate * jax.nn.softplus(-lam.astype(jnp.float32))
    a = jnp.exp(log_a)
    mult = jnp.sqrt(-jnp.expm1(2.0 * log_a))
    b = mult * (i_gate * rec.astype(jnp.float32))
    _, h = lax.associative_scan(_linear_recurrence_combine, (a, b), axis=1)
    y = jax.nn.gelu(gate_branch, approximate=True) * h.astype(x.dtype)
    return jnp.einsum("bse,ed->bsd", y, w_out) + b_out


def stick_breaking_attention(x, w_qkv, w_o):
    bsz, seq, _ = x.shape
    qkv = jnp.einsum("bsd,de->bse", x, w_qkv).reshape(bsz, seq, 3, N_HEADS, HEAD_DIM)
    q = jnp.transpose(qkv[:, :, 0], (0, 2, 1, 3))
    k = jnp.transpose(qkv[:, :, 1], (0, 2, 1, 3))
    v = jnp.transpose(qkv[:, :, 2], (0, 2, 1, 3))
    scale = 1.0 / math.sqrt(HEAD_DIM)
    outs = []
    for blk in range(seq // Q_BLOCK):
        t0 = blk * Q_BLOCK
        t1 = t0 + Q_BLOCK
        qb = q[:, :, t0:t1]
        kb = k[:, :, :t1]
        vb = v[:, :, :t1]
        z = jnp.einsum("bhqd,bhkd->bhqk", qb, kb).astype(jnp.float32) * scale
        t_idx = t0 + jnp.arange(Q_BLOCK)[:, None]
        s_idx = jnp.arange(t1)[None, :]
        strict = s_idx < t_idx
        log_fail = jnp.where(strict, jax.nn.log_sigmoid(-z), 0.0)
        suffix = lax.cumsum(log_fail, axis=3, reverse=True) - log_fail
        weights = jnp.where(strict, jnp.exp(jax.nn.log_sigmoid(z) + suffix), 0.0)
        outs.append(jnp.einsum("bhqk,bhkd->bhqd", weights.astype(vb.dtype), vb))
    o = jnp.concatenate(outs, axis=2)
    o = jnp.transpose(o, (0, 2, 1, 3)).reshape(bsz, seq, N_HEADS * HEAD_DIM)
    return jnp.einsum("bse,ed->bsd", o, w_o)


def swiglu_ffn(x, w_in, w_out):
    gu = jnp.einsum("bsd,df->bsf", x, w_in)
    h = jax.nn.silu(gu[..., :D_FF]) * gu[..., D_FF:]
    return jnp.einsum("bsf,fd->bsd", h, w_out)


def setup_inputs(seed: int = 0) -> dict:
    key = jax.random.key(seed)
    ks = jax.random.split(key, 20)
    f32 = jnp.float32

    def nrm(k, shape, fan_in):
        return jax.random.normal(k, shape, f32) * (fan_in ** -0.5)

    x = jax.random.normal(ks[0], (BATCH, SEQ, D_MODEL), f32)
    mix_norm = 1.0 + 0.05 * jax.random.normal(ks[1], (DEPTH, D_MODEL), f32)
    ffn_norm = 1.0 + 0.05 * jax.random.normal(ks[2], (DEPTH, D_MODEL), f32)
    final_norm = 1.0 + 0.05 * jax.random.normal(ks[3], (D_MODEL,), f32)

    lru_w_in = nrm(ks[4], (N_A, D_MODEL, 2 * LRU_WIDTH), D_MODEL)
    lru_b_in = 0.02 * jax.random.normal(ks[5], (N_A, 2 * LRU_WIDTH), f32)
    lru_conv_w = nrm(ks[6], (N_A, CONV_WIDTH, LRU_WIDTH), CONV_WIDTH)
    lru_conv_b = 0.02 * jax.random.normal(ks[7], (N_A, LRU_WIDTH), f32)
    lru_gate_w = nrm(ks[8], (N_A, 2, N_LRU_BLOCKS, LRU_BLOCK, LRU_BLOCK), LRU_BLOCK)
    lru_gate_b = 0.02 * jax.random.normal(ks[9], (N_A, 2, LRU_WIDTH), f32)
    u = jax.random.uniform(ks[10], (N_A, LRU_WIDTH), f32, A_MIN, A_MAX)
    a0 = u ** (1.0 / LRU_C)
    lru_lambda = jnp.log(a0) - jnp.log1p(-a0)
    lru_w_out = nrm(ks[11], (N_A, LRU_WIDTH, D_MODEL), LRU_WIDTH)
    lru_b_out = 0.02 * jax.random.normal(ks[12], (N_A, D_MODEL), f32)

    attn_w_qkv = nrm(ks[13], (N_B, D_MODEL, 3 * N_HEADS * HEAD_DIM), D_MODEL)
    attn_w_o = nrm(ks[14], (N_B, N_HEADS * HEAD_DIM, D_MODEL), N_HEADS * HEAD_DIM)

    ffn_w_in = nrm(ks[15], (DEPTH, D_MODEL, 2 * D_FF), D_MODEL)
    ffn_w_out = nrm(ks[16], (DEPTH, D_FF, D_MODEL), D_FF)

    return {
        "x": x, "mix_norm": mix_norm, "ffn_norm": ffn_norm, "final_norm": final_norm,
        "lru_w_in": lru_w_in, "lru_b_in": lru_b_in, "lru_conv_w": lru_conv_w,
        "lru_conv_b": lru_conv_b, "lru_gate_w": lru_gate_w, "lru_gate_b": lru_gate_b,
        "lru_lambda": lru_lambda, "lru_w_out": lru_w_out, "lru_b_out": lru_b_out,
        "attn_w_qkv": attn_w_qkv, "attn_w_o": attn_w_o,
        "ffn_w_in": ffn_w_in, "ffn_w_out": ffn_w_out,
    }


def reference(x, mix_norm, ffn_norm, final_norm, lru_w_in, lru_b_in, lru_conv_w,
              lru_conv_b, lru_gate_w, lru_gate_b, lru_lambda, lru_w_out, lru_b_out,
              attn_w_qkv, attn_w_o, ffn_w_in, ffn_w_out):
    h = x
    for layer in range(DEPTH):
        mixer = layer % N_MIXERS
        j = layer // N_MIXERS
        hn = rms_norm(h, mix_norm[layer])
        if mixer == 0:
            mixed = rg_lru_block(hn, lru_w_in[j], lru_b_in[j], lru_conv_w[j], lru_conv_b[j],
                                 lru_gate_w[j], lru_gate_b[j], lru_lambda[j],
                                 lru_w_out[j], lru_b_out[j])
        else:
            mixed = stick_breaking_attention(hn, attn_w_qkv[j], attn_w_o[j])
        h = h + mixed
        h = h + swiglu_ffn(rms_norm(h, ffn_norm[layer]), ffn_w_in[layer], ffn_w_out[layer])
    return rms_norm(h, final_norm)
```

```python
import numpy as np
import ml_dtypes
from contextlib import ExitStack
import concourse.bass as bass
import concourse.mybir as mybir
from concourse.bass_utils import run_bass_kernel_spmd

F32 = mybir.dt.float32
BF16 = mybir.dt.bfloat16
AF = mybir.ActivationFunctionType
ALU = mybir.AluOpType

T = 4096
D = 1024
DFF = 2816
NJ = DFF // 128
EPS = 1e-6
NEG = -30000.0

V_MIX0, V_FFN0, V_MIX1, V_FFN1, V_FIN = 0, 8, 16, 24, 32
V_BIN = 40
V_CW = 56
V_CB = 88
V_GB = 96
V_LAM = 112
V_BO = 120
NV = 128


_UID = [0]
DBG_NHP = 8
DBG_NQB = 8


def _uname(n):
    _UID[0] += 1
    return "%s_%d" % (n, _UID[0])


class Buf:
    __slots__ = ("w", "r")

    def __init__(self):
        self.w = None
        self.r = {}


class DSem:
    def __init__(self, key):
        self.key = key
        self.val = 0


class Sched:
    ENG = ("pe", "act", "dve", "pool", "sp")
    BLK = {"pe": "tensor", "act": "scalar", "dve": "vector", "pool": "gpsimd", "sp": "sync"}

    def __init__(self, nc, es):
        self.nc = nc
        self.es = es
        self.sems = {}
        self.cnt = {}
        for e in ("pe", "act", "dve", "pool"):
            self.sems[e] = es.enter_context(nc.semaphore("s_" + e))
            self.cnt[e] = 0
        self.ops = {e: [] for e in self.ENG}
        self.waited = {e: {} for e in self.ENG}
        self.ndma = 0
        self.bar = {e: {} for e in self.ENG}

    def dsem(self):
        key = "d%d" % self.ndma
        self.ndma += 1
        self.sems[key] = self.es.enter_context(self.nc.semaphore("s_" + key))
        return DSem(key)

    def op(self, eng, fn, reads=(), writes=(), dma=None, ndma=1):
        deps = dict(self.bar[eng])
        self.bar[eng] = {}

        def add(ev):
            if ev is None:
                return
            k, v = ev
            if deps.get(k, 0) < v:
                deps[k] = v

        for b in reads:
            add(b.w)
        for b in writes:
            add(b.w)
            for k, v in b.r.items():
                add((k, v))
        if dma is not None:
            dma.val += 16 * ndma
            ev = (dma.key, dma.val)
        else:
            self.cnt[eng] += 1
            ev = (eng, self.cnt[eng])
        self.ops[eng].append((fn, deps, ev, dma is not None))
        for b in writes:
            b.w = ev
            b.r = {}
        for b in reads:
            if b.r.get(ev[0], 0) < ev[1]:
                b.r[ev[0]] = ev[1]
        return ev

    def barrier(self):
        allv = {}
        for e in ("pe", "act", "dve", "pool"):
            if self.cnt[e]:
                allv[e] = self.cnt[e]
        for k in self.sems:
            if k.startswith("d"):
                pass
        for k, v in self._dvals().items():
            allv[k] = v
        for e in self.ENG:
            self.bar[e] = dict(allv)

    def _dvals(self):
        out = {}
        for e in self.ENG:
            for (_, _, ev, isd) in self.ops[e]:
                if isd and out.get(ev[0], 0) < ev[1]:
                    out[ev[0]] = ev[1]
        for k, v in getattr(self, "_dv_prev", {}).items():
            if out.get(k, 0) < v:
                out[k] = v
        return out

    def emit(self, final_waits=True):
        dv = self._dvals()
        self._dv_prev = dv
        with self.nc.Block() as block:
            for e in self.ENG:
                ops = self.ops[e]

                def body(eng, ops=ops, e=e):
                    wt = self.waited[e]
                    for (fn, deps, ev, isd) in ops:
                        for k, v in deps.items():
                            if wt.get(k, 0) < v:
                                eng.wait_ge(self.sems[k], v)
                                wt[k] = v
                        insts = fn(eng)
                        if not isinstance(insts, (list, tuple)):
                            insts = [insts]
                        if isd:
                            for i in insts:
                                i.then_inc(self.sems[ev[0]], 16)
                        else:
                            insts[-1].then_inc(self.sems[ev[0]], 1)
                    if e == "sp" and final_waits:
                        for k, v in dv.items():
                            if wt.get(k, 0) < v:
                                eng.wait_ge(self.sems[k], v)
                                wt[k] = v

                if ops or e == "sp":
                    getattr(block, self.BLK[e])(body)
        self.ops = {e: [] for e in self.ENG}


class Psum:
    def __init__(self, nc, es):
        self.t = es.enter_context(nc.psum_tensor(_uname("ps"), [128, 8, 512], F32))
        self.bufs = [Buf() for _ in range(8)]
        self.i = 0

    def next(self, lo=0, hi=8):
        n = hi - lo
        b = lo + (self.i % n)
        self.i += 1
        return self.t[:, b, :], self.bufs[b]


def _rms_stats(S, PS, x_ap, x_bufs, sq, sq_buf, ones, cbuf, epscol, rs, rs_buf, rstd, rstd_buf, NT):
    S.op("act", lambda a: a.activation(out=sq, in_=x_ap, func=AF.Square), reads=x_bufs, writes=[sq_buf])
    ps, psb = PS.next()

    def mm(pe):
        last = None
        for c in range(8):
            last = pe.matmul(ps[:, 0:NT], ones, sq[:, c, :], start=(c == 0), stop=(c == 7))
        return last

    S.op("pe", mm, reads=[sq_buf, cbuf], writes=[psb])
    S.op("act", lambda a: a.activation(out=rs, in_=ps[:, 0:NT], func=AF.Ln, bias=epscol, scale=1.0 / D),
         reads=[psb, cbuf], writes=[rs_buf])
    S.op("act", lambda a: a.activation(out=rstd, in_=rs, func=AF.Exp, scale=-0.5),
         reads=[rs_buf], writes=[rstd_buf])


def _load_w_cast(S, dst_ap, src_ap, wbuf, dsem, nsplit=1):
    k = dst_ap.shape[1]
    parts = list(range(k))

    def fn(g):
        return [g.dma_start(out=dst_ap[:, a, :], in_=src_ap[:, a, :], max_dma_last_dim=4096) for a in parts]

    S.op("pool", fn, writes=[wbuf], dma=dsem, ndma=len(parts))


def phase_ffn(nc, S, src, dst, w_in_d, w_out_d, consts, gcol, fin_out=None, fcol=None, NT=256):
    ones, vecs, epscol, cbuf = consts
    with ExitStack() as es:
        PS = Psum(nc, es)
        sb = lambda name, shape, dt: es.enter_context(nc.sbuf_tensor(_uname(name), shape, dt))
        w1 = sb("w1", [128, 8, 2 * DFF], BF16)
        w2 = sb("w2", [128, NJ, D], BF16)
        xt = [sb("xt%d" % i, [128, 8, NT], F32) for i in range(2)]
        hn = [sb("hn%d" % i, [128, 8, NT], BF16) for i in range(2)]
        sq = sb("sq", [128, 8, NT], BF16)
        hm = sb("hm", [128, NJ, NT], BF16)
        sg = [sb("sg%d" % i, [128, NT], F32) for i in range(3)]
        rs = sb("rs", [128, NT], F32)
        rstd = sb("rstd", [128, NT], F32)
        if fin_out is not None:
            yo = [sb("yo%d" % i, [128, 8, NT], F32) for i in range(1)]
            yo_b = [Buf()]
            yo_s = [S.dsem()]
        w1b = [Buf() for _ in range(4)]
        w2b = [Buf() for _ in range(2)]
        xt_b = [Buf(), Buf()]
        xt_s = [S.dsem(), S.dsem()]
        st_s = [S.dsem(), S.dsem()]
        hn_b = [Buf(), Buf()]
        sq_b, hm_b = Buf(), [Buf() for _ in range(NJ)]
        sg_b = [Buf() for _ in range(3)]
        rs_b, rstd_b = Buf(), Buf()
        wsem = [S.dsem() for _ in range(6)]

        ntile = T // NT

        def load_x(i):
            b = i % 2
            S.op("sp", lambda q, i=i, b=b: q.dma_start(
                out=xt[b][:], in_=src[:, i * NT:(i + 1) * NT].rearrange("(c p) t -> p c t", p=128)),
                writes=[xt_b[b]], dma=xt_s[b])

        load_x(0)
        w_in_v = w_in_d.rearrange("(kc p) m -> p kc m", p=128)
        for qd in range(4):
            _load_w_cast(S, w1[:, 2 * qd:2 * qd + 2, :], w_in_v[:, 2 * qd:2 * qd + 2, :], w1b[qd], wsem[qd], nsplit=2)
        w_out_v = w_out_d.rearrange("(kc p) m -> p kc m", p=128)
        for hf in range(2):
            _load_w_cast(S, w2[:, 11 * hf:11 * hf + 11, :], w_out_v[:, 11 * hf:11 * hf + 11, :], w2b[hf], wsem[4 + hf], nsplit=1)

        def norm(i):
            b = i % 2
            x = xt[b]
            _rms_stats(S, PS, x[:], [xt_b[b]], sq[:], sq_b, ones, cbuf, epscol, rs[:], rs_b, rstd[:], rstd_b, NT)
            for c in range(8):
                S.op("dve", lambda v, c=c, x=x, b=b: v.scalar_tensor_tensor(
                    out=hn[b][:, c, :], in0=x[:, c, :], scalar=vecs[:, gcol + c:gcol + c + 1], in1=rstd[:],
                    op0=ALU.mult, op1=ALU.mult), reads=[xt_b[b], rstd_b, cbuf], writes=[hn_b[b]])

        norm(0)
        for i in range(ntile):
            b = i % 2
            x = xt[b]
            if i + 1 < ntile:
                load_x(i + 1)
            for j in range(NJ):
                pg, pgb = PS.next()
                pu, pub = PS.next()

                def mm1(pe, j=j, pg=pg, pu=pu, b=b):
                    last = None
                    for c in range(8):
                        last = pe.matmul(pg[:, 0:NT], w1[:, c, j * 128:(j + 1) * 128], hn[b][:, c, :],
                                         start=(c == 0), stop=(c == 7))
                    for c in range(8):
                        last = pe.matmul(pu[:, 0:NT], w1[:, c, DFF + j * 128:DFF + (j + 1) * 128], hn[b][:, c, :],
                                         start=(c == 0), stop=(c == 7))
                    return last

                S.op("pe", mm1, reads=[hn_b[b]] + w1b, writes=[pgb, pub])
                k = j % 3
                S.op("act", lambda a, pg=pg, k=k: a.activation(out=sg[k][:], in_=pg[:, 0:NT], func=AF.Silu),
                     reads=[pgb], writes=[sg_b[k]])
                S.op("dve", lambda v, pu=pu, k=k, j=j: v.tensor_tensor(
                    out=hm[:, j, :], in0=pu[:, 0:NT], in1=sg[k][:], op=ALU.mult),
                    reads=[pub, sg_b[k]], writes=[hm_b[j]])
            if i + 1 < ntile:
                norm(i + 1)
            for c in range(8):
                po, pob = PS.next()

                def mm2(pe, c=c, po=po):
                    last = None
                    for j in range(NJ):
                        last = pe.matmul(po[:, 0:NT], w2[:, j, c * 128:(c + 1) * 128], hm[:, j, :],
                                         start=(j == 0), stop=(j == NJ - 1))
                    return last

                S.op("pe", mm2, reads=hm_b + w2b, writes=[pob])
                S.op("dve", lambda v, c=c, po=po, x=x: v.tensor_tensor(
                    out=x[:, c, :], in0=po[:, 0:NT], in1=x[:, c, :], op=ALU.add),
                    reads=[pob], writes=[xt_b[b]])
            if fin_out is None:
                S.op("sp", lambda q, i=i, x=x: q.dma_start(
                    out=dst[:, i * NT:(i + 1) * NT].rearrange("(c p) t -> p c t", p=128), in_=x[:]),
                    reads=[xt_b[b]], dma=st_s[b])
            else:
                _rms_stats(S, PS, x[:], [xt_b[b]], sq[:], sq_b, ones, cbuf, epscol, rs[:], rs_b, rstd[:], rstd_b, NT)
                for c in range(8):
                    S.op("dve", lambda v, c=c, x=x: v.scalar_tensor_tensor(
                        out=yo[0][:, c, :], in0=x[:, c, :], scalar=vecs[:, fcol + c:fcol + c + 1], in1=rstd[:],
                        op0=ALU.mult, op1=ALU.mult), reads=[xt_b[b], rstd_b, cbuf], writes=[yo_b[0]])
                S.op("sp", lambda q, i=i: q.dma_start(
                    out=fin_out[:, i * NT:(i + 1) * NT].rearrange("(c p) t -> p c t", p=128), in_=yo[0][:]),
                    reads=[yo_b[0]], dma=yo_s[0])
        S.emit()


def phase_lru(nc, S, src, dst, w_in_d, gw_d, w_out_d, consts, NT=512):
    ones, vecs, epscol, cbuf = consts
    with ExitStack() as es:
        PS = Psum(nc, es)
        sb = lambda name, shape, dt: es.enter_context(nc.sbuf_tensor(_uname(name), shape, dt))
        w1 = sb("lw1", [128, 8, 2048], BF16)
        gw = sb("lgw", [128, 16, 256], BF16)
        w2 = sb("lw2", [128, 8, D], BF16)
        xt = sb("lxt", [128, 8, NT], F32)
        xr = [sb("lxr%d" % i, [128, NT], F32) for i in range(3)]
        hn = sb("lhn", [128, 8, NT], BF16)
        sq = sb("lsq", [128, 8, NT], BF16)
        rec = [sb("lrec%d" % i, [128, NT + 3], F32) for i in range(3)]
        halo = sb("lhalo", [128, 8, 3], F32)
        hst = sb("lhst", [128, 8], F32)
        rc = sb("lrc", [128, 8, NT], F32)
        rcb = sb("lrcb", [128, 8, NT], BF16)
        ug = sb("lug", [128, 8, NT], F32)
        thr = sb("lthr", [128, 8, NT], F32)
        thi = sb("lthi", [128, 8, NT], F32)
        ta = [sb("lta%d" % i, [128, NT], F32) for i in range(2)]
        ta2 = [sb("lta2%d" % i, [128, NT], F32) for i in range(2)]
        tb = [sb("ltb%d" % i, [128, NT], F32) for i in range(2)]
        thh = [sb("lthh%d" % i, [128, NT], F32) for i in range(2)]
        y = sb("ly", [128, 8, NT], BF16)
        rs = sb("lrs", [128, NT], F32)
        rstd = sb("lrstd", [128, NT], F32)
        pv = sb("lpv", [128, 64], F32)
        w1b, gwb, w2b = [Buf(), Buf()], Buf(), Buf()
        xt_b, xt_s = Buf(), S.dsem()
        xr_b = [Buf() for _ in range(3)]
        xr_s = [S.dsem() for _ in range(3)]
        hn_b, sq_b, rs_b, rstd_b = Buf(), Buf(), Buf(), Buf()
        rec_b = [Buf() for _ in range(3)]
        halo_b = [Buf() for _ in range(8)]
        hst_b = [Buf() for _ in range(8)]
        rc_b = [Buf() for _ in range(8)]
        rcb_b = [Buf() for _ in range(8)]
        ug_b = [Buf() for _ in range(8)]
        thr_b = [Buf() for _ in range(8)]
        thi_b = [Buf() for _ in range(8)]
        ta_b = [Buf(), Buf()]
        ta2_b = [Buf(), Buf()]
        tb_b = [Buf(), Buf()]
        thh_b = [Buf(), Buf()]
        y_b = [Buf() for _ in range(8)]
        pv_b = Buf()
        wsem = [S.dsem() for _ in range(4)]
        ntile = T // NT

        def load_x(i):
            S.op("sp", lambda q, i=i: q.dma_start(
                out=xt[:], in_=src[:, i * NT:(i + 1) * NT].rearrange("(c p) t -> p c t", p=128)),
                writes=[xt_b], dma=xt_s)

        load_x(0)
        w_in_v = w_in_d.rearrange("(kc p) m -> p kc m", p=128)
        for hf in range(2):
            _load_w_cast(S, w1[:, 4 * hf:4 * hf + 4, :], w_in_v[:, 4 * hf:4 * hf + 4, :], w1b[hf], wsem[hf], nsplit=2)
        _load_w_cast(S, gw[:], gw_d.rearrange("g n (kc p) d -> p (g n kc) d", p=128), gwb, wsem[2])
        _load_w_cast(S, w2[:], w_out_d.rearrange("(kc p) m -> p kc m", p=128), w2b, wsem[3], nsplit=2)

        S.op("dve", lambda v: v.memset(pv[:, 40:41], 1.0), writes=[pv_b])
        S.op("dve", lambda v: v.memset(halo[:], 0.0), writes=halo_b)
        S.op("dve", lambda v: v.memset(hst[:], 0.0), writes=hst_b)
        S.op("dve", lambda v: v.tensor_scalar(out=pv[:, 0:16], in0=vecs[:, V_GB:V_GB + 16], scalar1=0.5, scalar2=None,
                                               op0=ALU.mult), reads=[cbuf], writes=[pv_b])
        S.op("act", lambda a: a.activation(out=pv[:, 32:40], in_=vecs[:, V_LAM:V_LAM + 8], func=AF.Exp, scale=-1.0),
             reads=[cbuf], writes=[pv_b])
        S.op("act", lambda a: a.activation(out=pv[:, 32:40], in_=pv[:, 32:40], func=AF.Ln, bias=pv[:, 40:41], scale=1.0),
             writes=[pv_b])
        S.op("dve", lambda v: v.tensor_scalar(out=pv[:, 16:24], in0=pv[:, 32:40], scalar1=-4.0, scalar2=None,
                                               op0=ALU.mult), writes=[pv_b])
        S.op("dve", lambda v: v.tensor_scalar(out=pv[:, 24:32], in0=pv[:, 32:40], scalar1=-8.0, scalar2=None,
                                               op0=ALU.mult), writes=[pv_b])

        xri = [0]
        for i in range(ntile):
            t0 = i * NT
            _rms_stats(S, PS, xt[:], [xt_b], sq[:], sq_b, ones, cbuf, epscol, rs[:], rs_b, rstd[:], rstd_b, NT)
            for c in range(8):
                S.op("dve", lambda v, c=c: v.scalar_tensor_tensor(
                    out=hn[:, c, :], in0=xt[:, c, :], scalar=vecs[:, V_MIX0 + c:V_MIX0 + c + 1], in1=rstd[:],
                    op0=ALU.mult, op1=ALU.mult), reads=[xt_b, rstd_b, cbuf], writes=[hn_b])
            if i + 1 < ntile:
                load_x(i + 1)
            for c in range(8):
                ps, psb = PS.next()

                def mm(pe, c=c, ps=ps):
                    last = None
                    for k in range(8):
                        last = pe.matmul(ps[:, 0:NT], w1[:, k, D + c * 128:D + (c + 1) * 128], hn[:, k, :],
                                         start=(k == 0), stop=(k == 7))
                    return last

                S.op("pe", mm, reads=[hn_b] + w1b, writes=[psb])
                rk = (i * 8 + c) % 3
                r_ = rec[rk]
                S.op("act", lambda a, ps=ps, r_=r_, c=c: a.activation(
                    out=r_[:, 3:3 + NT], in_=ps[:, 0:NT], func=AF.Identity,
                    bias=vecs[:, V_BIN + 8 + c:V_BIN + 9 + c], scale=1.0), reads=[psb, cbuf], writes=[rec_b[rk]])
                S.op("pool", lambda g, r_=r_, c=c: g.tensor_copy(out=r_[:, 0:3], in_=halo[:, c, :]),
                     reads=[halo_b[c]], writes=[rec_b[rk]])
                S.op("pool", lambda g, r_=r_, c=c: g.tensor_copy(out=halo[:, c, :], in_=r_[:, NT:NT + 3]),
                     reads=[rec_b[rk]], writes=[halo_b[c]])
                S.op("dve", lambda g, r_=r_, c=c: g.tensor_scalar(
                    out=rc[:, c, :], in0=r_[:, 0:NT], scalar1=vecs[:, V_CW + c:V_CW + c + 1],
                    scalar2=vecs[:, V_CB + c:V_CB + c + 1], op0=ALU.mult, op1=ALU.add),
                    reads=[rec_b[rk], cbuf], writes=[rc_b[c]])
                for k in range(1, 4):
                    S.op("dve", lambda g, r_=r_, c=c, k=k: g.scalar_tensor_tensor(
                        out=rc[:, c, :], in0=r_[:, k:k + NT], scalar=vecs[:, V_CW + 8 * k + c:V_CW + 8 * k + c + 1],
                        in1=rc[:, c, :], op0=ALU.mult, op1=ALU.add), reads=[rec_b[rk], cbuf], writes=[rc_b[c]])
                S.op("act", lambda a, c=c: a.activation(out=rcb[:, c, :], in_=rc[:, c, :], func=AF.Copy),
                     reads=[rc_b[c]], writes=[rcb_b[c]])
            for c in range(8):
                ps, psb = PS.next()

                def mm(pe, c=c, ps=ps):
                    last = None
                    for k in range(8):
                        last = pe.matmul(ps[:, 0:NT], w1[:, k, c * 128:(c + 1) * 128], hn[:, k, :],
                                         start=(k == 0), stop=(k == 7))
                    return last

                S.op("pe", mm, reads=[hn_b] + w1b, writes=[psb])
                S.op("act", lambda a, ps=ps, c=c: a.activation(
                    out=ug[:, c, :], in_=ps[:, 0:NT], func=AF.Gelu_apprx_tanh,
                    bias=vecs[:, V_BIN + c:V_BIN + c + 1], scale=1.0), reads=[psb, cbuf], writes=[ug_b[c]])
            for g_ in range(2):
                for oc in range(8):
                    n, dc = oc // 2, oc % 2
                    ps, psb = PS.next()

                    def mm(pe, g_=g_, n=n, dc=dc, ps=ps):
                        last = None
                        for kc in range(2):
                            last = pe.matmul(ps[:, 0:NT], gw[:, (g_ * 4 + n) * 2 + kc, dc * 128:(dc + 1) * 128],
                                             rcb[:, n * 2 + kc, :], start=(kc == 0), stop=(kc == 1))
                        return last

                    S.op("pe", mm, reads=[rcb_b[n * 2], rcb_b[n * 2 + 1], gwb], writes=[psb])
                    dstt = thr if g_ == 0 else thi
                    dstb = thr_b if g_ == 0 else thi_b
                    S.op("act", lambda a, ps=ps, dstt=dstt, oc=oc, g_=g_: a.activation(
                        out=dstt[:, oc, :], in_=ps[:, 0:NT], func=AF.Tanh,
                        bias=pv[:, g_ * 8 + oc:g_ * 8 + oc + 1], scale=0.5), reads=[psb, pv_b], writes=[dstb[oc]])
            for c in range(8):
                k2 = c % 2
                S.op("act", lambda a, c=c, k2=k2: a.activation(out=ta[k2][:], in_=thr[:, c, :], func=AF.Exp,
                                                               bias=pv[:, 16 + c:17 + c], scale=pv[:, 16 + c:17 + c]),
                     reads=[thr_b[c], pv_b], writes=[ta_b[k2]])
                S.op("act", lambda a, c=c, k2=k2: a.activation(out=ta2[k2][:], in_=thr[:, c, :], func=AF.Exp,
                                                               bias=pv[:, 24 + c:25 + c], scale=pv[:, 24 + c:25 + c]),
                     reads=[thr_b[c], pv_b], writes=[ta2_b[k2]])
                S.op("act", lambda a, k2=k2: a.activation(out=ta2[k2][:], in_=ta2[k2][:], func=AF.Ln,
                                                          bias=pv[:, 40:41], scale=-1.0),
                     reads=[pv_b], writes=[ta2_b[k2]])
                S.op("act", lambda a, k2=k2: a.activation(out=ta2[k2][:], in_=ta2[k2][:], func=AF.Exp, scale=0.5),
                     writes=[ta2_b[k2]])
                S.op("dve", lambda v, c=c, k2=k2: v.scalar_tensor_tensor(
                    out=tb[k2][:], in0=thi[:, c, :], scalar=1.0, in1=rc[:, c, :], op0=ALU.add, op1=ALU.mult),
                    reads=[thi_b[c], rc_b[c]], writes=[tb_b[k2]])
                S.op("dve", lambda v, k2=k2: v.scalar_tensor_tensor(
                    out=tb[k2][:], in0=tb[k2][:], scalar=0.5, in1=ta2[k2][:], op0=ALU.mult, op1=ALU.mult),
                    reads=[ta2_b[k2]], writes=[tb_b[k2]])
                S.op("dve", lambda v, c=c, k2=k2: v.tensor_tensor_scan(
                    out=thh[k2][:], data0=ta[k2][:], data1=tb[k2][:], initial=hst[:, c:c + 1],
                    op0=ALU.mult, op1=ALU.add), reads=[ta_b[k2], tb_b[k2], hst_b[c]], writes=[thh_b[k2]])
                S.op("dve", lambda v, c=c, k2=k2: v.tensor_copy(out=hst[:, c:c + 1], in_=thh[k2][:, NT - 1:NT]),
                     reads=[thh_b[k2]], writes=[hst_b[c]])
                S.op("dve", lambda v, c=c, k2=k2: v.tensor_tensor(
                    out=y[:, c, :], in0=ug[:, c, :], in1=thh[k2][:], op=ALU.mult),
                    reads=[ug_b[c], thh_b[k2]], writes=[y_b[c]])
            for c in range(8):
                k3 = xri[0] % 3
                xri[0] += 1
                S.op("sp", lambda q, c=c, k3=k3, t0=t0: q.dma_start(
                    out=xr[k3][:], in_=src[c * 128:(c + 1) * 128, t0:t0 + NT]), writes=[xr_b[k3]], dma=xr_s[k3])
                ps, psb = PS.next()

                def mm(pe, c=c, ps=ps):
                    last = None
                    for k in range(8):
                        last = pe.matmul(ps[:, 0:NT], w2[:, k, c * 128:(c + 1) * 128], y[:, k, :],
                                         start=(k == 0), stop=(k == 7))
                    return last

                S.op("pe", mm, reads=y_b + [w2b], writes=[psb])
                S.op("dve", lambda v, c=c, ps=ps, k3=k3: v.scalar_tensor_tensor(
                    out=xr[k3][:], in0=ps[:, 0:NT], scalar=vecs[:, V_BO + c:V_BO + c + 1], in1=xr[k3][:],
                    op0=ALU.add, op1=ALU.add), reads=[psb, cbuf], writes=[xr_b[k3]])
                S.op("sp", lambda q, c=c, k3=k3, t0=t0: q.dma_start(
                    out=dst[c * 128:(c + 1) * 128, t0:t0 + NT], in_=xr[k3][:]), reads=[xr_b[k3]], dma=xr_s[k3])
        S.emit()


def phase_qkv(nc, S, src, qT, kT, vd, w_d, consts, NT=512):
    ones, vecs, epscol, cbuf = consts
    with ExitStack() as es:
        PS = Psum(nc, es)
        sb = lambda name, shape, dt: es.enter_context(nc.sbuf_tensor(_uname(name), shape, dt))
        w = sb("qw", [128, 8, 3 * D], BF16)
        xt = [sb("qxt%d" % i, [128, 8, NT], F32) for i in range(2)]
        hn = sb("qhn", [128, 8, NT], BF16)
        sq = sb("qsq", [128, 8, NT], BF16)
        qst = [sb("qst%d" % i, [128, 8, NT], BF16) for i in range(2)]
        kst = [sb("kst%d" % i, [128, 8, NT], BF16) for i in range(2)]
        vst = [sb("vst%d" % i, [128, 4, D], BF16) for i in range(2)]
        rs = sb("qrs", [128, NT], F32)
        rstd = sb("qrstd", [128, NT], F32)
        wb = [Buf() for _ in range(3)]
        wsem = [S.dsem() for _ in range(3)]
        xt_b, xt_s = [Buf(), Buf()], [S.dsem(), S.dsem()]
        hn_b, sq_b, rs_b, rstd_b = Buf(), Buf(), Buf(), Buf()
        qst_b, kst_b, vst_b = [Buf(), Buf()], [Buf(), Buf()], [Buf(), Buf()]
        qst_s, kst_s, vst_s = [S.dsem(), S.dsem()], [S.dsem(), S.dsem()], [S.dsem(), S.dsem()]
        ntile = T // NT

        def load_x(i):
            b = i % 2
            S.op("sp", lambda q, i=i, b=b: q.dma_start(
                out=xt[b][:], in_=src[:, i * NT:(i + 1) * NT].rearrange("(c p) t -> p c t", p=128)),
                writes=[xt_b[b]], dma=xt_s[b])

        load_x(0)
        w_v = w_d.rearrange("(kc p) m -> p kc m", p=128)
        for part in range(3):
            def fn(g, part=part):
                return [g.dma_start(out=w[:, k, part * D:(part + 1) * D], in_=w_v[:, k, part * D:(part + 1) * D],
                                    max_dma_last_dim=4096) for k in range(8)]
            S.op("pool", fn, writes=[wb[part]], dma=wsem[part], ndma=8)

        for i in range(ntile):
            b = i % 2
            x = xt[b]
            if i + 1 < ntile:
                load_x(i + 1)
            _rms_stats(S, PS, x[:], [xt_b[b]], sq[:], sq_b, ones, cbuf, epscol, rs[:], rs_b, rstd[:], rstd_b, NT)
            for c in range(8):
                S.op("dve", lambda v, c=c, x=x: v.scalar_tensor_tensor(
                    out=hn[:, c, :], in0=x[:, c, :], scalar=vecs[:, V_MIX1 + c:V_MIX1 + c + 1], in1=rstd[:],
                    op0=ALU.mult, op1=ALU.mult), reads=[xt_b[b], rstd_b, cbuf], writes=[hn_b])
            for part, (stg, stg_b, scale) in enumerate(((qst, qst_b, 0.125), (kst, kst_b, 1.0))):
                for c in range(8):
                    ps, psb = PS.next()

                    def mm(pe, c=c, ps=ps, part=part):
                        last = None
                        for k in range(8):
                            last = pe.matmul(ps[:, 0:NT], w[:, k, part * D + c * 128:part * D + (c + 1) * 128],
                                             hn[:, k, :], start=(k == 0), stop=(k == 7))
                        return last

                    S.op("pe", mm, reads=[hn_b, wb[part]], writes=[psb])
                    if c % 2 == 0:
                        S.op("act", lambda a, ps=ps, stg=stg, c=c, scale=scale, b=b: a.activation(
                            out=stg[b][:, c, :], in_=ps[:, 0:NT], func=AF.Copy, scale=scale),
                            reads=[psb], writes=[stg_b[b]])
                    else:
                        S.op("dve", lambda v, ps=ps, stg=stg, c=c, scale=scale, b=b: v.tensor_scalar(
                            out=stg[b][:, c, :], in0=ps[:, 0:NT], scalar1=scale, scalar2=None, op0=ALU.mult),
                            reads=[psb], writes=[stg_b[b]])
                dd = qT if part == 0 else kT
                S.op("sp", lambda q, dd=dd, stg=stg, i=i, b=b: q.dma_start(
                    out=dd[:, i * NT:(i + 1) * NT].rearrange("(c p) t -> p c t", p=128), in_=stg[b][:]),
                    reads=[stg_b[b]], dma=(qst_s if part == 0 else kst_s)[b])
            for tb in range(4):
                for fh in range(2):
                    ps, psb = PS.next()

                    def mm(pe, tb=tb, fh=fh, ps=ps):
                        last = None
                        for k in range(8):
                            last = pe.matmul(ps[:, :], hn[:, k, tb * 128:(tb + 1) * 128],
                                             w[:, k, 2 * D + fh * 512:2 * D + (fh + 1) * 512],
                                             start=(k == 0), stop=(k == 7))
                        return last

                    S.op("pe", mm, reads=[hn_b, wb[2]], writes=[psb])
                    if fh == 0:
                        S.op("act", lambda a, ps=ps, tb=tb, fh=fh, b=b: a.activation(
                            out=vst[b][:, tb, fh * 512:(fh + 1) * 512], in_=ps[:, :], func=AF.Copy),
                            reads=[psb], writes=[vst_b[b]])
                    else:
                        S.op("dve", lambda v, ps=ps, tb=tb, fh=fh, b=b: v.tensor_copy(
                            out=vst[b][:, tb, fh * 512:(fh + 1) * 512], in_=ps[:, :]),
                            reads=[psb], writes=[vst_b[b]])
            S.op("sp", lambda q, i=i, b=b: q.dma_start(out=vd[:, 4 * i:4 * i + 4, :], in_=vst[b][:]),
                 reads=[vst_b[b]], dma=vst_s[b])
        S.emit()


def phase_attn(nc, S, qT, kT, vd, oT, cm_d, consts):
    ones, vecs, epscol, cbuf = consts
    with ExitStack() as es:
        sb = lambda name, shape, dt: es.enter_context(nc.sbuf_tensor(_uname(name), shape, dt))
        pst = es.enter_context(nc.psum_tensor(_uname("aps"), [128, 8, 512], F32))
        cm = sb("acm", [128, 3 * 128 + 4 * 512], BF16)
        ident = cm[:, 0:128]
        ntri = cm[:, 128:256]
        nones = cm[:, 256:384]
        kt = [sb("akt%d" % i, [128, T], BF16) for i in range(2)]
        qz = [[sb("aqz%d_%d" % (i, h), [128, T], BF16) for h in range(2)] for i in range(2)]
        vv = [sb("avv%d" % i, [128, 32, 128], BF16) for i in range(2)]
        eb = [sb("aeb%d" % i, [128, 512], F32) for i in range(2)]
        spb = [sb("asp%d" % i, [128, 512], BF16) for i in range(4)]
        wwb = [sb("aww%d" % i, [128, 512], BF16) for i in range(3)]
        rab = [sb("ara%d" % i, [128, 512], BF16) for i in range(4)]
        ot = [sb("aot%d" % i, [128, 512], BF16) for i in range(2)]
        onec = sb("aone", [128, 1], F32)
        cm_b, one_b = Buf(), Buf()
        kt_b, vv_b = [Buf(), Buf()], [Buf(), Buf()]
        qz_b = [[Buf(), Buf()], [Buf(), Buf()]]
        ld_s = [S.dsem(), S.dsem()]
        eb_b = [Buf() for _ in range(2)]
        sp_b = [Buf() for _ in range(4)]
        ww_b = [Buf() for _ in range(3)]
        ra_b = [Buf() for _ in range(4)]
        ot_b, ot_s = [Buf(), Buf()], [S.dsem(), S.dsem()]
        A_b = [Buf(), Buf()]
        B_b = [Buf(), Buf()]
        O_b = [Buf(), Buf(), Buf()]
        cs = S.dsem()
        S.op("sp", lambda q: q.dma_start(out=cm[:], in_=cm_d), writes=[cm_b], dma=cs)
        S.op("dve", lambda v: v.memset(onec[:], 1.0), writes=[one_b])
        for i in range(2):
            S.op("pool", lambda g, i=i: g.memset(qz[i][0][64:128, :], 0.0), writes=[qz_b[i][0]])
            S.op("pool", lambda g, i=i: g.memset(qz[i][1][0:64, :], 0.0), writes=[qz_b[i][1]])

        def load_hp(hp):
            pb = hp % 2

            def fn(q, hp=hp, pb=pb):
                return [q.dma_start(out=kt[pb][:], in_=kT[hp * 128:(hp + 1) * 128, :]),
                        q.dma_start(out=qz[pb][0][0:64, :], in_=qT[hp * 128:hp * 128 + 64, :]),
                        q.dma_start(out=qz[pb][1][64:128, :], in_=qT[hp * 128 + 64:(hp + 1) * 128, :]),
                        q.dma_start(out=vv[pb][:], in_=vd[:, :, hp * 128:(hp + 1) * 128])]

            S.op("sp", fn, writes=[kt_b[pb], qz_b[pb][0], qz_b[pb][1], vv_b[pb]], dma=ld_s[pb], ndma=4)

        load_hp(0)
        octr = [0]
        for hp in range(DBG_NHP):
            pb = hp % 2
            if hp + 1 < DBG_NHP:
                load_hp(hp + 1)
            tiles = []
            for qb in range(DBG_NQB):
                for h in range(2):
                    kbs = list(range(4 * qb + 3, -1, -1))
                    for n, kb in enumerate(kbs):
                        tiles.append(dict(qb=qb, h=h, kb=kb, first=(n == 0), last=(n == len(kbs) - 1),
                                          dlt=(kb - 4 * qb) if kb >= 4 * qb else None))
            NTL = len(tiles)
            rstate = {"cur": None, "n": 0}
            obank = {}

            def emitA(i):
                t = tiles[i]
                bank = i % 2
                A = pst[:, bank, :]

                def mm(pe, t=t, A=A, pb=pb):
                    last = pe.matmul(A, kt[pb][:, t["kb"] * 128:(t["kb"] + 1) * 128],
                                     qz[pb][t["h"]][:, t["qb"] * 512:(t["qb"] + 1) * 512],
                                     start=True, stop=(t["dlt"] is None))
                    if t["dlt"] is not None:
                        last = pe.matmul(A, ident, cm[:, 384 + t["dlt"] * 512:384 + (t["dlt"] + 1) * 512],
                                         start=False, stop=True)
                    return last

                S.op("pe", mm, reads=[kt_b[pb], qz_b[pb][t["h"]], cm_b], writes=[A_b[bank]])
                e = eb[i % 2]
                S.op("act", lambda a, A=A, e=e: a.activation(out=e[:], in_=A, func=AF.Exp),
                     reads=[A_b[bank]], writes=[eb_b[i % 2]])
                sp = spb[i % 4]
                S.op("act", lambda a, e=e, sp=sp: a.activation(out=sp[:], in_=e[:], func=AF.Ln, bias=onec[:], scale=1.0),
                     reads=[eb_b[i % 2], one_b], writes=[sp_b[i % 4]])
                t["racc"] = None if t["first"] else rstate["cur"]
                if not t["last"]:
                    k = rstate["n"] % 4
                    rstate["n"] += 1
                    if t["first"]:
                        S.op("pool", lambda g, k=k, sp=sp: g.tensor_copy(out=rab[k][:], in_=sp[:]),
                             reads=[sp_b[i % 4]], writes=[ra_b[k]])
                    else:
                        pk = rstate["cur"]
                        S.op("pool", lambda g, k=k, sp=sp, pk=pk: g.tensor_tensor(
                            out=rab[k][:], in0=rab[pk][:], in1=sp[:], op=ALU.add),
                            reads=[sp_b[i % 4], ra_b[pk]], writes=[ra_b[k]])
                    rstate["cur"] = k

            def emitB(i):
                t = tiles[i]
                bank = 2 + (i % 2)
                Bk = pst[:, bank, :]
                sp = spb[i % 4]
                rk = t["racc"]

                def mm(pe, t=t, Bk=Bk, sp=sp, rk=rk, pb=pb):
                    pe.matmul(Bk, kt[pb][:, t["kb"] * 128:(t["kb"] + 1) * 128],
                              qz[pb][t["h"]][:, t["qb"] * 512:(t["qb"] + 1) * 512], start=True, stop=False)
                    if t["dlt"] is not None:
                        pe.matmul(Bk, ident, cm[:, 384 + t["dlt"] * 512:384 + (t["dlt"] + 1) * 512],
                                  start=False, stop=False)
                    last = pe.matmul(Bk, ntri, sp[:], start=False, stop=(rk is None))
                    if rk is not None:
                        last = pe.matmul(Bk, nones, rab[rk][:], start=False, stop=True)
                    return last

                rd = [kt_b[pb], qz_b[pb][t["h"]], cm_b, sp_b[i % 4]]
                if rk is not None:
                    rd.append(ra_b[rk])
                S.op("pe", mm, reads=rd, writes=[B_b[i % 2]])
                ww = wwb[i % 3]
                S.op("act", lambda a, Bk=Bk, ww=ww: a.activation(out=ww[:], in_=Bk, func=AF.Exp),
                     reads=[B_b[i % 2]], writes=[ww_b[i % 3]])

            def emitPV(i):
                t = tiles[i]
                if t["first"]:
                    obank[(t["qb"], t["h"])] = octr[0] % 3
                    octr[0] += 1
                ob = obank[(t["qb"], t["h"])]
                O = pst[:, 4 + ob, :]
                ww = wwb[i % 3]
                S.op("pe", lambda pe, t=t, O=O, ww=ww, pb=pb: pe.matmul(
                    O, vv[pb][:, t["kb"], :], ww[:], start=t["first"], stop=t["last"]),
                    reads=[vv_b[pb], ww_b[i % 3]], writes=[O_b[ob]])
                if t["last"]:
                    h, qb = t["h"], t["qb"]
                    ok = qb % 2
                    S.op("dve", lambda v, O=O, h=h, ok=ok: v.tensor_copy(
                        out=ot[ok][h * 64:(h + 1) * 64, :], in_=O[h * 64:(h + 1) * 64, :]),
                        reads=[O_b[ob]], writes=[ot_b[ok]])
                    if h == 1:
                        S.op("sp", lambda q, ok=ok, qb=qb, hp=hp: q.dma_start(
                            out=oT[hp * 128:(hp + 1) * 128, qb * 512:(qb + 1) * 512], in_=ot[ok][:]),
                            reads=[ot_b[ok]], dma=ot_s[ok])

            emitA(0)
            if NTL > 1:
                emitA(1)
            for i in range(NTL):
                emitB(i)
                if i + 2 < NTL:
                    emitA(i + 2)
                emitPV(i)
        S.emit()


def phase_oproj(nc, S, src, oT, dst, w_d, consts, NT=512):
    with ExitStack() as es:
        PS = Psum(nc, es)
        sb = lambda name, shape, dt: es.enter_context(nc.sbuf_tensor(_uname(name), shape, dt))
        w = sb("ow", [128, 8, D], BF16)
        ob = [sb("oo%d" % i, [128, 8, NT], BF16) for i in range(2)]
        xr = [sb("oxr%d" % i, [128, NT], F32) for i in range(4)]
        w_b, w_s = Buf(), S.dsem()
        ob_b, ob_s = [Buf(), Buf()], [S.dsem(), S.dsem()]
        xr_b, xr_s = [Buf() for _ in range(4)], [S.dsem() for _ in range(4)]
        ntile = T // NT
        _load_w_cast(S, w[:], w_d.rearrange("(kc p) m -> p kc m", p=128), w_b, w_s)

        def load_o(i):
            b = i % 2
            S.op("sp", lambda q, i=i, b=b: q.dma_start(
                out=ob[b][:], in_=oT[:, i * NT:(i + 1) * NT].rearrange("(c p) t -> p c t", p=128)),
                writes=[ob_b[b]], dma=ob_s[b])

        load_o(0)
        n = 0
        for i in range(ntile):
            b = i % 2
            t0 = i * NT
            if i + 1 < ntile:
                load_o(i + 1)
            for c in range(8):
                k3 = n % 4
                n += 1
                S.op("sp", lambda q, c=c, k3=k3, t0=t0: q.dma_start(
                    out=xr[k3][:], in_=src[c * 128:(c + 1) * 128, t0:t0 + NT]), writes=[xr_b[k3]], dma=xr_s[k3])
                ps, psb = PS.next()

                def mm(pe, c=c, ps=ps, b=b):
                    last = None
                    for k in range(8):
                        last = pe.matmul(ps[:, 0:NT], w[:, k, c * 128:(c + 1) * 128], ob[b][:, k, :],
                                         start=(k == 0), stop=(k == 7))
                    return last

                S.op("pe", mm, reads=[ob_b[b], w_b], writes=[psb])
                S.op("dve", lambda v, ps=ps, k3=k3: v.tensor_tensor(
                    out=xr[k3][:], in0=ps[:, 0:NT], in1=xr[k3][:], op=ALU.add), reads=[psb], writes=[xr_b[k3]])
                S.op("sp", lambda q, c=c, k3=k3, t0=t0: q.dma_start(
                    out=dst[c * 128:(c + 1) * 128, t0:t0 + NT], in_=xr[k3][:]), reads=[xr_b[k3]], dma=xr_s[k3])
        S.emit()


def _consts(nc, S, es, vecs_d, ones_d):
    sb = lambda name, shape, dt: es.enter_context(nc.sbuf_tensor(_uname(name), shape, dt))
    ones = sb("ones_sb", [128, 128], BF16)
    vecs = sb("vecs_sb", [128, NV], F32)
    epsc = sb("epsc", [128, 1], F32)
    cbuf = Buf()
    ds = S.dsem()
    S.op("sp", lambda q: [q.dma_start(out=ones[:], in_=ones_d), q.dma_start(out=vecs[:], in_=vecs_d)],
         writes=[cbuf], dma=ds, ndma=2)
    S.op("dve", lambda v: v.memset(epsc[:], EPS), writes=[cbuf])
    return ones[:], vecs, epsc[:], cbuf


def build(phases, fused):
    nc = bass.Bass("TRN2", target_bir_lowering=False)
    es = ExitStack()
    names_in, names_out = [], []

    cache = {}

    def dram(name, shape, dt, produced_by, consumed_by=None):
        if name not in cache:
            cache[name] = _dram(name, shape, dt, produced_by)
        return cache[name]

    def _dram(name, shape, dt, produced_by):
        if produced_by is None or produced_by not in phases:
            kind = "ExternalInput"
            names_in.append(name)
        elif name == "yT" or (not fused):
            kind = "ExternalOutput"
            names_out.append(name)
        else:
            kind = "Internal"
        return nc.dram_tensor(name, shape, dt, kind=kind).ap()

    vecs_d = dram("vecs", [128, NV], F32, None)
    ones_d = dram("ones", [128, 128], BF16, None)
    S = Sched(nc, es)
    consts = _consts(nc, S, es, vecs_d, ones_d)
    if "lru" in phases:
        src = dram("xT", [D, T], F32, None)
        dst = dram("h1", [D, T], F32, "lru")
        w_in = dram("lru_w_in", [D, 2 * D], F32, None)
        gw_d = dram("lru_gate_w", [2, 4, 256, 256], F32, None)
        w_out = dram("lru_w_out", [D, D], F32, None)
        phase_lru(nc, S, src, dst, w_in, gw_d, w_out, consts)
    if "ffn0" in phases:
        src = dram("h1", [D, T], F32, "lru")
        dst = dram("h2", [D, T], F32, "ffn0")
        w_in = dram("ffn_w_in0", [D, 2 * DFF], F32, None)
        w_out = dram("ffn_w_out0", [DFF, D], F32, None)
        phase_ffn(nc, S, src, dst, w_in, w_out, consts, V_FFN0)
    if "attn" in phases:
        src = dram("h2", [D, T], F32, "ffn0")
        cm_d = dram("cmask", [128, 3 * 128 + 4 * 512], BF16, None)
        qT = dram("qT", [D, T], BF16, "attn")
        kT = dram("kT", [D, T], BF16, "attn")
        vd = dram("vd", [128, 32, D], BF16, "attn")
        oT = dram("oT", [D, T], BF16, "attn")
        dst = dram("h3", [D, T], F32, "attn")
        w_qkv = dram("attn_w_qkv", [D, 3 * D], F32, None)
        w_o = dram("attn_w_o", [D, D], F32, None)
        phase_qkv(nc, S, src, qT, kT, vd, w_qkv, consts)
        phase_attn(nc, S, qT, kT, vd, oT, cm_d, consts)
        phase_oproj(nc, S, src, oT, dst, w_o, consts)
    if "ffn1" in phases:
        src = dram("h3", [D, T], F32, "attn")
        dst = dram("yT", [D, T], F32, "ffn1")
        w_in = dram("ffn_w_in1", [D, 2 * DFF], F32, None)
        w_out = dram("ffn_w_out1", [DFF, D], F32, None)
        phase_ffn(nc, S, src, None, w_in, w_out, consts, V_FFN1, fin_out=dst, fcol=V_FIN)
    es.close()
    return nc, names_in, names_out


def pack_vecs(inp):
    def cols(v):
        v = np.asarray(v, dtype=np.float32).reshape(-1, 128)
        return v.T
    parts = [cols(inp["mix_norm"][0]), cols(inp["ffn_norm"][0]), cols(inp["mix_norm"][1]),
             cols(inp["ffn_norm"][1]), cols(inp["final_norm"]),
             cols(inp["lru_b_in"][0]), cols(inp["lru_conv_w"][0]), cols(inp["lru_conv_b"][0]),
             cols(inp["lru_gate_b"][0]), cols(inp["lru_lambda"][0]), cols(inp["lru_b_out"][0])]
    out = np.ascontiguousarray(np.concatenate(parts, axis=1))
    assert out.shape == (128, NV), out.shape
    return out


def make_cmask():
    j = np.arange(128)[:, None]
    sidx = np.arange(128)[None, :]
    ident = (j == sidx).astype(np.float32)
    ntri = -(j >= sidx).astype(np.float32)
    nones = -np.ones((128, 128), np.float32)
    f = np.arange(512)[None, :]
    p = np.arange(128)[:, None]
    masks = [np.where(f - p > 128 * d, 0.0, NEG).astype(np.float32) for d in range(4)]
    return np.ascontiguousarray(np.concatenate([ident, ntri, nones] + masks, axis=1)).astype(ml_dtypes.bfloat16)


LAUNCHES = [["lru", "ffn0", "attn", "ffn1"]]


def kernel(**inputs):
    inp = {k: np.asarray(v) for k, v in inputs.items()}
    x = inp["x"].astype(np.float32, copy=False)
    n = x.shape[0]
    shared = {
        "vecs": pack_vecs(inp),
        "ones": np.ones((128, 128), dtype=ml_dtypes.bfloat16),
        "cmask": make_cmask(),
        "lru_w_in": np.ascontiguousarray(inp["lru_w_in"][0], dtype=np.float32),
        "lru_gate_w": np.ascontiguousarray(inp["lru_gate_w"][0], dtype=np.float32),
        "lru_w_out": np.ascontiguousarray(inp["lru_w_out"][0], dtype=np.float32),
        "attn_w_qkv": np.ascontiguousarray(inp["attn_w_qkv"][0], dtype=np.float32),
        "attn_w_o": np.ascontiguousarray(inp["attn_w_o"][0], dtype=np.float32),
        "ffn_w_in0": np.ascontiguousarray(inp["ffn_w_in"][0], dtype=np.float32),
        "ffn_w_in1": np.ascontiguousarray(inp["ffn_w_in"][1], dtype=np.float32),
        "ffn_w_out0": np.ascontiguousarray(inp["ffn_w_out"][0], dtype=np.float32),
        "ffn_w_out1": np.ascontiguousarray(inp["ffn_w_out"][1], dtype=np.float32),
    }
    per_core = [{"xT": np.ascontiguousarray(x[c].T)} for c in range(n)]
    fused = len(LAUNCHES) == 1
    for phases in LAUNCHES:
        nc, nin, nout = build(phases, fused)
        in_maps = []
        for c in range(n):
            m = {}
            for name in nin:
                m[name] = shared[name] if name in shared else per_core[c][name]
            in_maps.append(m)
        res = run_bass_kernel_spmd(nc, in_maps, core_ids=list(range(n)))
        for c in range(n):
            for name in nout:
                per_core[c][name] = res.results[c][name]
    out = np.stack([np.ascontiguousarray(per_core[c]["yT"].T) for c in range(n)], axis=0)
    return out.astype(np.float32, copy=False)
```

```python
import numpy as np
import ml_dtypes
from contextlib import ExitStack
import concourse.bass as bass
import concourse.mybir as mybir
from concourse.bass_utils import run_bass_kernel_spmd

F32 = mybir.dt.float32
BF16 = mybir.dt.bfloat16
AF = mybir.ActivationFunctionType
ALU = mybir.AluOpType

T = 4096
D = 1024
DFF = 2816
NJ = DFF // 128
EPS = 1e-6
NEG = -30000.0

V_MIX0, V_FFN0, V_MIX1, V_FFN1, V_FIN = 0, 8, 16, 24, 32
V_BIN = 40
V_CW = 56
V_CB = 88
V_GB = 96
V_LAM = 112
V_BO = 120
NV = 128


_UID = [0]
DBG_NHP = 8
DBG_NQB = 8


def _uname(n):
    _UID[0] += 1
    return "%s_%d" % (n, _UID[0])


class Buf:
    __slots__ = ("w", "r")

    def __init__(self):
        self.w = None
        self.r = {}


class DSem:
    def __init__(self, key):
        self.key = key
        self.val = 0


class Sched:
    ENG = ("pe", "act", "dve", "pool", "sp")
    BLK = {"pe": "tensor", "act": "scalar", "dve": "vector", "pool": "gpsimd", "sp": "sync"}

    def __init__(self, nc, es):
        self.nc = nc
        self.es = es
        self.sems = {}
        self.cnt = {}
        for e in ("pe", "act", "dve", "pool"):
            self.sems[e] = es.enter_context(nc.semaphore("s_" + e))
            self.cnt[e] = 0
        self.ops = {e: [] for e in self.ENG}
        self.waited = {e: {} for e in self.ENG}
        self.ndma = 0
        self.bar = {e: {} for e in self.ENG}

    def dsem(self):
        key = "d%d" % self.ndma
        self.ndma += 1
        self.sems[key] = self.es.enter_context(self.nc.semaphore("s_" + key))
        return DSem(key)

    def op(self, eng, fn, reads=(), writes=(), dma=None, ndma=1):
        deps = dict(self.bar[eng])
        self.bar[eng] = {}

        def add(ev):
            if ev is None:
                return
            k, v = ev
            if deps.get(k, 0) < v:
                deps[k] = v

        for b in reads:
            add(b.w)
        for b in writes:
            add(b.w)
            for k, v in b.r.items():
                add((k, v))
        if dma is not None:
            dma.val += 16 * ndma
            ev = (dma.key, dma.val)
        else:
            self.cnt[eng] += 1
            ev = (eng, self.cnt[eng])
        self.ops[eng].append((fn, deps, ev, dma is not None))
        for b in writes:
            b.w = ev
            b.r = {}
        for b in reads:
            if b.r.get(ev[0], 0) < ev[1]:
                b.r[ev[0]] = ev[1]
        return ev

    def barrier(self):
        allv = {}
        for e in ("pe", "act", "dve", "pool"):
            if self.cnt[e]:
                allv[e] = self.cnt[e]
        for k in self.sems:
            if k.startswith("d"):
                pass
        for k, v in self._dvals().items():
            allv[k] = v
        for e in self.ENG:
            self.bar[e] = dict(allv)

    def _dvals(self):
        out = {}
        for e in self.ENG:
            for (_, _, ev, isd) in self.ops[e]:
                if isd and out.get(ev[0], 0) < ev[1]:
                    out[ev[0]] = ev[1]
        for k, v in getattr(self, "_dv_prev", {}).items():
            if out.get(k, 0) < v:
                out[k] = v
        return out

    def emit(self, final_waits=True):
        dv = self._dvals()
        self._dv_prev = dv
        with self.nc.Block() as block:
            for e in self.ENG:
                ops = self.ops[e]

                def body(eng, ops=ops, e=e):
                    wt = self.waited[e]
                    for (fn, deps, ev, isd) in ops:
                        for k, v in deps.items():
                            if wt.get(k, 0) < v:
                                eng.wait_ge(self.sems[k], v)
                                wt[k] = v
                        insts = fn(eng)
                        if not isinstance(insts, (list, tuple)):
                            insts = [insts]
                        if isd:
                            for i in insts:
                                i.then_inc(self.sems[ev[0]], 16)
                        else:
                            insts[-1].then_inc(self.sems[ev[0]], 1)
                    if e == "sp" and final_waits:
                        for k, v in dv.items():
                            if wt.get(k, 0) < v:
                                eng.wait_ge(self.sems[k], v)
                                wt[k] = v

                if ops or e == "sp":
                    getattr(block, self.BLK[e])(body)
        self.ops = {e: [] for e in self.ENG}


class Psum:
    def __init__(self, nc, es):
        self.t = es.enter_context(nc.psum_tensor(_uname("ps"), [128, 8, 512], F32))
        self.bufs = [Buf() for _ in range(8)]
        self.i = 0

    def next(self, lo=0, hi=8):
        n = hi - lo
        b = lo + (self.i % n)
        self.i += 1
        return self.t[:, b, :], self.bufs[b]


def _rms_stats(S, PS, x_ap, x_bufs, sq, sq_buf, ones, cbuf, epscol, rs, rs_buf, rstd, rstd_buf, NT):
    S.op("act", lambda a: a.activation(out=sq, in_=x_ap, func=AF.Square), reads=x_bufs, writes=[sq_buf])
    ps, psb = PS.next()

    def mm(pe):
        last = None
        for c in range(8):
            last = pe.matmul(ps[:, 0:NT], ones, sq[:, c, :], start=(c == 0), stop=(c == 7))
        return last

    S.op("pe", mm, reads=[sq_buf, cbuf], writes=[psb])
    S.op("act", lambda a: a.activation(out=rs, in_=ps[:, 0:NT], func=AF.Ln, bias=epscol, scale=1.0 / D),
         reads=[psb, cbuf], writes=[rs_buf])
    S.op("act", lambda a: a.activation(out=rstd, in_=rs, func=AF.Exp, scale=-0.5),
         reads=[rs_buf], writes=[rstd_buf])


def _load_w_cast(S, dst_ap, src_ap, wbuf, dsem, nsplit=1):
    k = dst_ap.shape[1]
    parts = list(range(k))

    def fn(g):
        return [g.dma_start(out=dst_ap[:, a, :], in_=src_ap[:, a, :], max_dma_last_dim=4096) for a in parts]

    S.op("pool", fn, writes=[wbuf], dma=dsem, ndma=len(parts))


def phase_ffn(nc, S, src, dst, w_in_d, w_out_d, consts, gcol, fin_out=None, fcol=None, NT=256):
    ones, vecs, epscol, cbuf = consts
    with ExitStack() as es:
        PS = Psum(nc, es)
        sb = lambda name, shape, dt: es.enter_context(nc.sbuf_tensor(_uname(name), shape, dt))
        w1 = sb("w1", [128, 8, 2 * DFF], BF16)
        w2 = sb("w2", [128, NJ, D], BF16)
        xt = [sb("xt%d" % i, [128, 8, NT], F32) for i in range(2)]
        hn = [sb("hn%d" % i, [128, 8, NT], BF16) for i in range(2)]
        sq = sb("sq", [128, 8, NT], BF16)
        hm = sb("hm", [128, NJ, NT], BF16)
        sg = [sb("sg%d" % i, [128, NT], F32) for i in range(3)]
        rs = sb("rs", [128, NT], F32)
        rstd = sb("rstd", [128, NT], F32)
        if fin_out is not None:
            yo = [sb("yo%d" % i, [128, 8, NT], F32) for i in range(1)]
            yo_b = [Buf()]
            yo_s = [S.dsem()]
        w1b = [Buf() for _ in range(4)]
        w2b = [Buf() for _ in range(2)]
        xt_b = [Buf(), Buf()]
        xt_s = [S.dsem(), S.dsem()]
        st_s = [S.dsem(), S.dsem()]
        hn_b = [Buf(), Buf()]
        sq_b, hm_b = Buf(), [Buf() for _ in range(NJ)]
        sg_b = [Buf() for _ in range(3)]
        rs_b, rstd_b = Buf(), Buf()
        wsem = [S.dsem() for _ in range(6)]

        ntile = T // NT

        def load_x(i):
            b = i % 2
            S.op("sp", lambda q, i=i, b=b: q.dma_start(
                out=xt[b][:], in_=src[:, i * NT:(i + 1) * NT].rearrange("(c p) t -> p c t", p=128)),
                writes=[xt_b[b]], dma=xt_s[b])

        load_x(0)
        w_in_v = w_in_d.rearrange("(kc p) m -> p kc m", p=128)
        for qd in range(4):
            _load_w_cast(S, w1[:, 2 * qd:2 * qd + 2, :], w_in_v[:, 2 * qd:2 * qd + 2, :], w1b[qd], wsem[qd], nsplit=2)
        w_out_v = w_out_d.rearrange("(kc p) m -> p kc m", p=128)
        for hf in range(2):
            _load_w_cast(S, w2[:, 11 * hf:11 * hf + 11, :], w_out_v[:, 11 * hf:11 * hf + 11, :], w2b[hf], wsem[4 + hf], nsplit=1)

        def norm(i):
            b = i % 2
            x = xt[b]
            _rms_stats(S, PS, x[:], [xt_b[b]], sq[:], sq_b, ones, cbuf, epscol, rs[:], rs_b, rstd[:], rstd_b, NT)
            for c in range(8):
                S.op("dve", lambda v, c=c, x=x, b=b: v.scalar_tensor_tensor(
                    out=hn[b][:, c, :], in0=x[:, c, :], scalar=vecs[:, gcol + c:gcol + c + 1], in1=rstd[:],
                    op0=ALU.mult, op1=ALU.mult), reads=[xt_b[b], rstd_b, cbuf], writes=[hn_b[b]])

        norm(0)
        for i in range(ntile):
            b = i % 2
            x = xt[b]
            if i + 1 < ntile:
                load_x(i + 1)
            for j in range(NJ):
                pg, pgb = PS.next()
                pu, pub = PS.next()

                def mm1(pe, j=j, pg=pg, pu=pu, b=b):
                    last = None
                    for c in range(8):
                        last = pe.matmul(pg[:, 0:NT], w1[:, c, j * 128:(j + 1) * 128], hn[b][:, c, :],
                                         start=(c == 0), stop=(c == 7))
                    for c in range(8):
                        last = pe.matmul(pu[:, 0:NT], w1[:, c, DFF + j * 128:DFF + (j + 1) * 128], hn[b][:, c, :],
                                         start=(c == 0), stop=(c == 7))
                    return last

                S.op("pe", mm1, reads=[hn_b[b]] + w1b, writes=[pgb, pub])
                k = j % 3
                S.op("act", lambda a, pg=pg, k=k: a.activation(out=sg[k][:], in_=pg[:, 0:NT], func=AF.Silu),
                     reads=[pgb], writes=[sg_b[k]])
                S.op("dve", lambda v, pu=pu, k=k, j=j: v.tensor_tensor(
                    out=hm[:, j, :], in0=pu[:, 0:NT], in1=sg[k][:], op=ALU.mult),
                    reads=[pub, sg_b[k]], writes=[hm_b[j]])
            if i + 1 < ntile:
                norm(i + 1)
            for c in range(8):
                po, pob = PS.next()

                def mm2(pe, c=c, po=po):
                    last = None
                    for j in range(NJ):
                        last = pe.matmul(po[:, 0:NT], w2[:, j, c * 128:(c + 1) * 128], hm[:, j, :],
                                         start=(j == 0), stop=(j == NJ - 1))
                    return last

                S.op("pe", mm2, reads=hm_b + w2b, writes=[pob])
                S.op("dve", lambda v, c=c, po=po, x=x: v.tensor_tensor(
                    out=x[:, c, :], in0=po[:, 0:NT], in1=x[:, c, :], op=ALU.add),
                    reads=[pob], writes=[xt_b[b]])
            if fin_out is None:
                S.op("sp", lambda q, i=i, x=x: q.dma_start(
                    out=dst[:, i * NT:(i + 1) * NT].rearrange("(c p) t -> p c t", p=128), in_=x[:]),
                    reads=[xt_b[b]], dma=st_s[b])
            else:
                _rms_stats(S, PS, x[:], [xt_b[b]], sq[:], sq_b, ones, cbuf, epscol, rs[:], rs_b, rstd[:], rstd_b, NT)
                for c in range(8):
                    S.op("dve", lambda v, c=c, x=x: v.scalar_tensor_tensor(
                        out=yo[0][:, c, :], in0=x[:, c, :], scalar=vecs[:, fcol + c:fcol + c + 1], in1=rstd[:],
                        op0=ALU.mult, op1=ALU.mult), reads=[xt_b[b], rstd_b, cbuf], writes=[yo_b[0]])
                S.op("sp", lambda q, i=i: q.dma_start(
                    out=fin_out[:, i * NT:(i + 1) * NT].rearrange("(c p) t -> p c t", p=128), in_=yo[0][:]),
                    reads=[yo_b[0]], dma=yo_s[0])
        S.emit()


def phase_lru(nc, S, src, dst, w_in_d, gw_d, w_out_d, consts, NT=512):
    ones, vecs, epscol, cbuf = consts
    with ExitStack() as es:
        PS = Psum(nc, es)
        sb = lambda name, shape, dt: es.enter_context(nc.sbuf_tensor(_uname(name), shape, dt))
        w1 = sb("lw1", [128, 8, 2048], BF16)
        gw = sb("lgw", [128, 16, 256], BF16)
        w2 = sb("lw2", [128, 8, D], BF16)
        xt = sb("lxt", [128, 8, NT], F32)
        xr = [sb("lxr%d" % i, [128, NT], F32) for i in range(3)]
        hn = sb("lhn", [128, 8, NT], BF16)
        sq = sb("lsq", [128, 8, NT], BF16)
        rec = [sb("lrec%d" % i, [128, NT + 3], F32) for i in range(3)]
        halo = sb("lhalo", [128, 8, 3], F32)
        hst = sb("lhst", [128, 8], F32)
        rc = sb("lrc", [128, 8, NT], F32)
        rcb = sb("lrcb", [128, 8, NT], BF16)
        ug = sb("lug", [128, 8, NT], F32)
        thr = sb("lthr", [128, 8, NT], F32)
        thi = sb("lthi", [128, 8, NT], F32)
        ta = [sb("lta%d" % i, [128, NT], F32) for i in range(2)]
        ta2 = [sb("lta2%d" % i, [128, NT], F32) for i in range(2)]
        tb = [sb("ltb%d" % i, [128, NT], F32) for i in range(2)]
        thh = [sb("lthh%d" % i, [128, NT], F32) for i in range(2)]
        y = sb("ly", [128, 8, NT], BF16)
        rs = sb("lrs", [128, NT], F32)
        rstd = sb("lrstd", [128, NT], F32)
        pv = sb("lpv", [128, 64], F32)
        w1b, gwb, w2b = [Buf(), Buf()], Buf(), Buf()
        xt_b, xt_s = Buf(), S.dsem()
        xr_b = [Buf() for _ in range(3)]
        xr_s = [S.dsem() for _ in range(3)]
        hn_b, sq_b, rs_b, rstd_b = Buf(), Buf(), Buf(), Buf()
        rec_b = [Buf() for _ in range(3)]
        halo_b = [Buf() for _ in range(8)]
        hst_b = [Buf() for _ in range(8)]
        rc_b = [Buf() for _ in range(8)]
        rcb_b = [Buf() for _ in range(8)]
        ug_b = [Buf() for _ in range(8)]
        thr_b = [Buf() for _ in range(8)]
        thi_b = [Buf() for _ in range(8)]
        ta_b = [Buf(), Buf()]
        ta2_b = [Buf(), Buf()]
        tb_b = [Buf(), Buf()]
        thh_b = [Buf(), Buf()]
        y_b = [Buf() for _ in range(8)]
        pv_b = Buf()
        wsem = [S.dsem() for _ in range(4)]
        ntile = T // NT

        def load_x(i):
            S.op("sp", lambda q, i=i: q.dma_start(
                out=xt[:], in_=src[:, i * NT:(i + 1) * NT].rearrange("(c p) t -> p c t", p=128)),
                writes=[xt_b], dma=xt_s)

        load_x(0)
        w_in_v = w_in_d.rearrange("(kc p) m -> p kc m", p=128)
        for hf in range(2):
            _load_w_cast(S, w1[:, 4 * hf:4 * hf + 4, :], w_in_v[:, 4 * hf:4 * hf + 4, :], w1b[hf], wsem[hf], nsplit=2)
        _load_w_cast(S, gw[:], gw_d.rearrange("g n (kc p) d -> p (g n kc) d", p=128), gwb, wsem[2])
        _load_w_cast(S, w2[:], w_out_d.rearrange("(kc p) m -> p kc m", p=128), w2b, wsem[3], nsplit=2)

        S.op("dve", lambda v: v.memset(pv[:, 40:41], 1.0), writes=[pv_b])
        S.op("dve", lambda v: v.memset(halo[:], 0.0), writes=halo_b)
        S.op("dve", lambda v: v.memset(hst[:], 0.0), writes=hst_b)
        S.op("dve", lambda v: v.tensor_scalar(out=pv[:, 0:16], in0=vecs[:, V_GB:V_GB + 16], scalar1=0.5, scalar2=None,
                                               op0=ALU.mult), reads=[cbuf], writes=[pv_b])
        S.op("act", lambda a: a.activation(out=pv[:, 32:40], in_=vecs[:, V_LAM:V_LAM + 8], func=AF.Exp, scale=-1.0),
             reads=[cbuf], writes=[pv_b])
        S.op("act", lambda a: a.activation(out=pv[:, 32:40], in_=pv[:, 32:40], func=AF.Ln, bias=pv[:, 40:41], scale=1.0),
             writes=[pv_b])
        S.op("dve", lambda v: v.tensor_scalar(out=pv[:, 16:24], in0=pv[:, 32:40], scalar1=-4.0, scalar2=None,
                                               op0=ALU.mult), writes=[pv_b])
        S.op("dve", lambda v: v.tensor_scalar(out=pv[:, 24:32], in0=pv[:, 32:40], scalar1=-8.0, scalar2=None,
                                               op0=ALU.mult), writes=[pv_b])

        xri = [0]

        def norm(i):
            _rms_stats(S, PS, xt[:], [xt_b], sq[:], sq_b, ones, cbuf, epscol, rs[:], rs_b, rstd[:], rstd_b, NT)
            for c in range(8):
                S.op("dve", lambda v, c=c: v.scalar_tensor_tensor(
                    out=hn[:, c, :], in0=xt[:, c, :], scalar=vecs[:, V_MIX0 + c:V_MIX0 + c + 1], in1=rstd[:],
                    op0=ALU.mult, op1=ALU.mult), reads=[xt_b, rstd_b, cbuf], writes=[hn_b])
            if i + 1 < ntile:
                load_x(i + 1)

        def s1(i, kk):
            if kk < 8:
                c = kk
                ps, psb = PS.next()

                def mm(pe, c=c, ps=ps):
                    last = None
                    for k in range(8):
                        last = pe.matmul(ps[:, 0:NT], w1[:, k, D + c * 128:D + (c + 1) * 128], hn[:, k, :],
                                         start=(k == 0), stop=(k == 7))
                    return last

                S.op("pe", mm, reads=[hn_b] + w1b, writes=[psb])
                rk = (i * 8 + c) % 3
                r_ = rec[rk]
                S.op("act", lambda a, ps=ps, r_=r_, c=c: a.activation(
                    out=r_[:, 3:3 + NT], in_=ps[:, 0:NT], func=AF.Identity,
                    bias=vecs[:, V_BIN + 8 + c:V_BIN + 9 + c], scale=1.0), reads=[psb, cbuf], writes=[rec_b[rk]])
                S.op("pool", lambda g, r_=r_, c=c: g.tensor_copy(out=r_[:, 0:3], in_=halo[:, c, :]),
                     reads=[halo_b[c]], writes=[rec_b[rk]])
                S.op("pool", lambda g, r_=r_, c=c: g.tensor_copy(out=halo[:, c, :], in_=r_[:, NT:NT + 3]),
                     reads=[rec_b[rk]], writes=[halo_b[c]])
                S.op("dve", lambda g, r_=r_, c=c: g.tensor_scalar(
                    out=rc[:, c, :], in0=r_[:, 0:NT], scalar1=vecs[:, V_CW + c:V_CW + c + 1],
                    scalar2=vecs[:, V_CB + c:V_CB + c + 1], op0=ALU.mult, op1=ALU.add),
                    reads=[rec_b[rk], cbuf], writes=[rc_b[c]])
                for k in range(1, 4):
                    S.op("dve", lambda g, r_=r_, c=c, k=k: g.scalar_tensor_tensor(
                        out=rc[:, c, :], in0=r_[:, k:k + NT], scalar=vecs[:, V_CW + 8 * k + c:V_CW + 8 * k + c + 1],
                        in1=rc[:, c, :], op0=ALU.mult, op1=ALU.add), reads=[rec_b[rk], cbuf], writes=[rc_b[c]])
                S.op("act", lambda a, c=c: a.activation(out=rcb[:, c, :], in_=rc[:, c, :], func=AF.Copy),
                     reads=[rc_b[c]], writes=[rcb_b[c]])
            else:
                c = kk - 8
                ps, psb = PS.next()

                def mm(pe, c=c, ps=ps):
                    last = None
                    for k in range(8):
                        last = pe.matmul(ps[:, 0:NT], w1[:, k, c * 128:(c + 1) * 128], hn[:, k, :],
                                         start=(k == 0), stop=(k == 7))
                    return last

                S.op("pe", mm, reads=[hn_b] + w1b, writes=[psb])
                S.op("act", lambda a, ps=ps, c=c: a.activation(
                    out=ug[:, c, :], in_=ps[:, 0:NT], func=AF.Gelu_apprx_tanh,
                    bias=vecs[:, V_BIN + c:V_BIN + c + 1], scale=1.0), reads=[psb, cbuf], writes=[ug_b[c]])
                oc = c
                n, dc = oc // 2, oc % 2
                for g_ in range(2):
                    ps, psb = PS.next()

                    def mm(pe, g_=g_, n=n, dc=dc, ps=ps):
                        last = None
                        for kc in range(2):
                            last = pe.matmul(ps[:, 0:NT], gw[:, (g_ * 4 + n) * 2 + kc, dc * 128:(dc + 1) * 128],
                                             rcb[:, n * 2 + kc, :], start=(kc == 0), stop=(kc == 1))
                        return last

                    S.op("pe", mm, reads=[rcb_b[n * 2], rcb_b[n * 2 + 1], gwb], writes=[psb])
                    dstt = thr if g_ == 0 else thi
                    dstb = thr_b if g_ == 0 else thi_b
                    S.op("act", lambda a, ps=ps, dstt=dstt, oc=oc, g_=g_: a.activation(
                        out=dstt[:, oc, :], in_=ps[:, 0:NT], func=AF.Tanh,
                        bias=pv[:, g_ * 8 + oc:g_ * 8 + oc + 1], scale=0.5), reads=[psb, pv_b], writes=[dstb[oc]])

        def s2(i, kk):
            t0 = i * NT
            if kk < 8:
                c = kk
                k2 = c % 2
                S.op("act", lambda a, c=c, k2=k2: a.activation(out=ta[k2][:], in_=thr[:, c, :], func=AF.Exp,
                                                               bias=pv[:, 16 + c:17 + c], scale=pv[:, 16 + c:17 + c]),
                     reads=[thr_b[c], pv_b], writes=[ta_b[k2]])
                S.op("act", lambda a, c=c, k2=k2: a.activation(out=ta2[k2][:], in_=thr[:, c, :], func=AF.Exp,
                                                               bias=pv[:, 24 + c:25 + c], scale=pv[:, 24 + c:25 + c]),
                     reads=[thr_b[c], pv_b], writes=[ta2_b[k2]])
                S.op("act", lambda a, k2=k2: a.activation(out=ta2[k2][:], in_=ta2[k2][:], func=AF.Ln,
                                                          bias=pv[:, 40:41], scale=-1.0),
                     reads=[pv_b], writes=[ta2_b[k2]])
                S.op("act", lambda a, k2=k2: a.activation(out=ta2[k2][:], in_=ta2[k2][:], func=AF.Exp, scale=0.5),
                     writes=[ta2_b[k2]])
                S.op("dve", lambda v, c=c, k2=k2: v.scalar_tensor_tensor(
                    out=tb[k2][:], in0=thi[:, c, :], scalar=1.0, in1=rc[:, c, :], op0=ALU.add, op1=ALU.mult),
                    reads=[thi_b[c], rc_b[c]], writes=[tb_b[k2]])
                S.op("dve", lambda v, k2=k2: v.scalar_tensor_tensor(
                    out=tb[k2][:], in0=tb[k2][:], scalar=0.5, in1=ta2[k2][:], op0=ALU.mult, op1=ALU.mult),
                    reads=[ta2_b[k2]], writes=[tb_b[k2]])
                S.op("dve", lambda v, c=c, k2=k2: v.tensor_tensor_scan(
                    out=thh[k2][:], data0=ta[k2][:], data1=tb[k2][:], initial=hst[:, c:c + 1],
                    op0=ALU.mult, op1=ALU.add), reads=[ta_b[k2], tb_b[k2], hst_b[c]], writes=[thh_b[k2]])
                S.op("dve", lambda v, c=c, k2=k2: v.tensor_copy(out=hst[:, c:c + 1], in_=thh[k2][:, NT - 1:NT]),
                     reads=[thh_b[k2]], writes=[hst_b[c]])
                S.op("dve", lambda v, c=c, k2=k2: v.tensor_tensor(
                    out=y[:, c, :], in0=ug[:, c, :], in1=thh[k2][:], op=ALU.mult),
                    reads=[ug_b[c], thh_b[k2]], writes=[y_b[c]])
            else:
                c = kk - 8
                k3 = xri[0] % 3
                xri[0] += 1
                S.op("sp", lambda q, c=c, k3=k3, t0=t0: q.dma_start(
                    out=xr[k3][:], in_=src[c * 128:(c + 1) * 128, t0:t0 + NT]), writes=[xr_b[k3]], dma=xr_s[k3])
                ps, psb = PS.next()

                def mm(pe, c=c, ps=ps):
                    last = None
                    for k in range(8):
                        last = pe.matmul(ps[:, 0:NT], w2[:, k, c * 128:(c + 1) * 128], y[:, k, :],
                                         start=(k == 0), stop=(k == 7))
                    return last

                S.op("pe", mm, reads=y_b + [w2b], writes=[psb])
                S.op("dve", lambda v, c=c, ps=ps, k3=k3: v.scalar_tensor_tensor(
                    out=xr[k3][:], in0=ps[:, 0:NT], scalar=vecs[:, V_BO + c:V_BO + c + 1], in1=xr[k3][:],
                    op0=ALU.add, op1=ALU.add), reads=[psb, cbuf], writes=[xr_b[k3]])
                S.op("sp", lambda q, c=c, k3=k3, t0=t0: q.dma_start(
                    out=dst[c * 128:(c + 1) * 128, t0:t0 + NT], in_=xr[k3][:]), reads=[xr_b[k3]], dma=xr_s[k3])

        norm(0)
        for kk in range(16):
            s1(0, kk)
        for i in range(ntile):
            if i + 1 < ntile:
                norm(i + 1)
            for kk in range(16):
                s2(i, kk)
                if i + 1 < ntile:
                    s1(i + 1, kk)
        S.emit()


def phase_qkv(nc, S, src, qT, kT, vd, w_d, consts, NT=512):
    ones, vecs, epscol, cbuf = consts
    with ExitStack() as es:
        PS = Psum(nc, es)
        sb = lambda name, shape, dt: es.enter_context(nc.sbuf_tensor(_uname(name), shape, dt))
        w = sb("qw", [128, 8, 3 * D], BF16)
        xt = [sb("qxt%d" % i, [128, 8, NT], F32) for i in range(2)]
        hn = sb("qhn", [128, 8, NT], BF16)
        sq = sb("qsq", [128, 8, NT], BF16)
        qst = [sb("qst%d" % i, [128, 8, NT], BF16) for i in range(2)]
        kst = [sb("kst%d" % i, [128, 8, NT], BF16) for i in range(2)]
        vst = [sb("vst%d" % i, [128, 4, D], BF16) for i in range(2)]
        rs = sb("qrs", [128, NT], F32)
        rstd = sb("qrstd", [128, NT], F32)
        wb = [Buf() for _ in range(3)]
        wsem = [S.dsem() for _ in range(3)]
        xt_b, xt_s = [Buf(), Buf()], [S.dsem(), S.dsem()]
        hn_b, sq_b, rs_b, rstd_b = Buf(), Buf(), Buf(), Buf()
        qst_b, kst_b, vst_b = [Buf(), Buf()], [Buf(), Buf()], [Buf(), Buf()]
        qst_s, kst_s, vst_s = [S.dsem(), S.dsem()], [S.dsem(), S.dsem()], [S.dsem(), S.dsem()]
        ntile = T // NT

        def load_x(i):
            b = i % 2
            S.op("sp", lambda q, i=i, b=b: q.dma_start(
                out=xt[b][:], in_=src[:, i * NT:(i + 1) * NT].rearrange("(c p) t -> p c t", p=128)),
                writes=[xt_b[b]], dma=xt_s[b])

        load_x(0)
        w_v = w_d.rearrange("(kc p) m -> p kc m", p=128)
        for part in range(3):
            def fn(g, part=part):
                return [g.dma_start(out=w[:, k, part * D:(part + 1) * D], in_=w_v[:, k, part * D:(part + 1) * D],
                                    max_dma_last_dim=4096) for k in range(8)]
            S.op("pool", fn, writes=[wb[part]], dma=wsem[part], ndma=8)

        for i in range(ntile):
            b = i % 2
            x = xt[b]
            if i + 1 < ntile:
                load_x(i + 1)
            _rms_stats(S, PS, x[:], [xt_b[b]], sq[:], sq_b, ones, cbuf, epscol, rs[:], rs_b, rstd[:], rstd_b, NT)
            for c in range(8):
                S.op("dve", lambda v, c=c, x=x: v.scalar_tensor_tensor(
                    out=hn[:, c, :], in0=x[:, c, :], scalar=vecs[:, V_MIX1 + c:V_MIX1 + c + 1], in1=rstd[:],
                    op0=ALU.mult, op1=ALU.mult), reads=[xt_b[b], rstd_b, cbuf], writes=[hn_b])
            for part, (stg, stg_b, scale) in enumerate(((qst, qst_b, 0.125), (kst, kst_b, 1.0))):
                for c in range(8):
                    ps, psb = PS.next()

                    def mm(pe, c=c, ps=ps, part=part):
                        last = None
                        for k in range(8):
                            last = pe.matmul(ps[:, 0:NT], w[:, k, part * D + c * 128:part * D + (c + 1) * 128],
                                             hn[:, k, :], start=(k == 0), stop=(k == 7))
                        return last

                    S.op("pe", mm, reads=[hn_b, wb[part]], writes=[psb])
                    if c % 2 == 0:
                        S.op("act", lambda a, ps=ps, stg=stg, c=c, scale=scale, b=b: a.activation(
                            out=stg[b][:, c, :], in_=ps[:, 0:NT], func=AF.Copy, scale=scale),
                            reads=[psb], writes=[stg_b[b]])
                    else:
                        S.op("dve", lambda v, ps=ps, stg=stg, c=c, scale=scale, b=b: v.tensor_scalar(
                            out=stg[b][:, c, :], in0=ps[:, 0:NT], scalar1=scale, scalar2=None, op0=ALU.mult),
                            reads=[psb], writes=[stg_b[b]])
                dd = qT if part == 0 else kT
                S.op("sp", lambda q, dd=dd, stg=stg, i=i, b=b: q.dma_start(
                    out=dd[:, i * NT:(i + 1) * NT].rearrange("(c p) t -> p c t", p=128), in_=stg[b][:]),
                    reads=[stg_b[b]], dma=(qst_s if part == 0 else kst_s)[b])
            for tb in range(4):
                for fh in range(2):
                    ps, psb = PS.next()

                    def mm(pe, tb=tb, fh=fh, ps=ps):
                        last = None
                        for k in range(8):
                            last = pe.matmul(ps[:, :], hn[:, k, tb * 128:(tb + 1) * 128],
                                             w[:, k, 2 * D + fh * 512:2 * D + (fh + 1) * 512],
                                             start=(k == 0), stop=(k == 7))
                        return last

                    S.op("pe", mm, reads=[hn_b, wb[2]], writes=[psb])
                    if fh == 0:
                        S.op("act", lambda a, ps=ps, tb=tb, fh=fh, b=b: a.activation(
                            out=vst[b][:, tb, fh * 512:(fh + 1) * 512], in_=ps[:, :], func=AF.Copy),
                            reads=[psb], writes=[vst_b[b]])
                    else:
                        S.op("dve", lambda v, ps=ps, tb=tb, fh=fh, b=b: v.tensor_copy(
                            out=vst[b][:, tb, fh * 512:(fh + 1) * 512], in_=ps[:, :]),
                            reads=[psb], writes=[vst_b[b]])
            S.op("sp", lambda q, i=i, b=b: q.dma_start(out=vd[:, 4 * i:4 * i + 4, :], in_=vst[b][:]),
                 reads=[vst_b[b]], dma=vst_s[b])
        S.emit()


def phase_attn(nc, S, qT, kT, vd, oT, cm_d, consts):
    ones, vecs, epscol, cbuf = consts
    with ExitStack() as es:
        sb = lambda name, shape, dt: es.enter_context(nc.sbuf_tensor(_uname(name), shape, dt))
        pst = es.enter_context(nc.psum_tensor(_uname("aps"), [128, 8, 512], F32))
        cm = sb("acm", [128, 3 * 128 + 4 * 512], BF16)
        ident = cm[:, 0:128]
        ntri = cm[:, 128:256]
        nones = cm[:, 256:384]
        mask0 = cm[:, 384:384 + 512]
        kt = [sb("akt%d" % i, [128, T], BF16) for i in range(2)]
        qz = [[sb("aqz%d_%d" % (i, h), [128, T], BF16) for h in range(2)] for i in range(2)]
        vv = [sb("avv%d" % i, [128, 32, 128], BF16) for i in range(2)]
        spb = [sb("asp%d" % i, [128, 2, 512], BF16) for i in range(3)]
        wwb = [sb("aww%d" % i, [128, 2, 512], BF16) for i in range(3)]
        rab = [sb("ara%d" % i, [128, 2, 512], BF16) for i in range(3)]
        ot = [sb("aot%d" % i, [128, 512], BF16) for i in range(2)]
        zt = sb("azt", [128, 512], BF16)
        onec = sb("aone", [128, 1], F32)
        cm_b, one_b = Buf(), Buf()
        kt_b, vv_b = [Buf(), Buf()], [Buf(), Buf()]
        qz_b = [[Buf(), Buf()], [Buf(), Buf()]]
        ld_s = [S.dsem(), S.dsem()]
        sp_b = [Buf() for _ in range(3)]
        ww_b = [Buf() for _ in range(3)]
        ra_b = [Buf() for _ in range(3)]
        ot_b, ot_s = [Buf(), Buf()], [S.dsem(), S.dsem()]
        A_b = Buf()
        B_b = [Buf(), Buf()]
        O_b = Buf()
        cs = S.dsem()
        S.op("sp", lambda q: q.dma_start(out=cm[:], in_=cm_d), writes=[cm_b], dma=cs)
        S.op("dve", lambda v: v.memset(onec[:], 1.0), writes=[one_b])
        S.op("dve", lambda v: v.memset(zt[:], 0.0), writes=[one_b])
        for i in range(2):
            S.op("pool", lambda g, i=i: g.memset(qz[i][0][64:128, :], 0.0), writes=[qz_b[i][0]])
            S.op("pool", lambda g, i=i: g.memset(qz[i][1][0:64, :], 0.0), writes=[qz_b[i][1]])

        def load_hp(hp):
            pb = hp % 2

            def fn(q, hp=hp, pb=pb):
                return [q.dma_start(out=kt[pb][:], in_=kT[hp * 128:(hp + 1) * 128, :]),
                        q.dma_start(out=qz[pb][0][0:64, :], in_=qT[hp * 128:hp * 128 + 64, :]),
                        q.dma_start(out=qz[pb][1][64:128, :], in_=qT[hp * 128 + 64:(hp + 1) * 128, :]),
                        q.dma_start(out=vv[pb][:], in_=vd[:, :, hp * 128:(hp + 1) * 128])]

            S.op("sp", fn, writes=[kt_b[pb], qz_b[pb][0], qz_b[pb][1], vv_b[pb]], dma=ld_s[pb], ndma=4)

        Abank = pst[:, 0:2, :]
        Bbank = [pst[:, 2:4, :], pst[:, 4:6, :]]
        Obank = pst[:, 6:8, :]
        load_hp(0)
        gi = [0]
        rn = [0]
        for hp in range(DBG_NHP):
            pb = hp % 2
            if hp + 1 < DBG_NHP:
                load_hp(hp + 1)
            tiles = []
            for qb in range(DBG_NQB):
                kbs = list(range(4 * qb + 3, -1, -1))
                for n, kb in enumerate(kbs):
                    dlt = (kb - 4 * qb) if kb >= 4 * qb else None
                    c0 = 128 * dlt if dlt is not None else 0
                    tiles.append(dict(qb=qb, kb=kb, first=(n == 0), last=(n == len(kbs) - 1), dlt=dlt, c0=c0,
                                      N=512 - c0, g=gi[0]))
                    gi[0] += 1
            NTL = len(tiles)
            rstate = {"cur": None}

            def qk(pe, out3, t, pb, final_stop):
                last = None
                N, c0 = t["N"], t["c0"]
                q0 = t["qb"] * 512 + c0
                for h in range(2):
                    last = pe.matmul(out3[:, h, 0:N], kt[pb][:, t["kb"] * 128:(t["kb"] + 1) * 128],
                                     qz[pb][h][:, q0:q0 + N], start=True,
                                     stop=(final_stop and t["dlt"] is None))
                    if t["dlt"] is not None:
                        last = pe.matmul(out3[:, h, 0:N], ident, mask0[:, 0:N], start=False, stop=final_stop)
                return last

            def emitA_pe(i):
                t = tiles[i]
                S.op("pe", lambda pe, t=t, pb=pb: qk(pe, Abank, t, pb, True),
                     reads=[kt_b[pb], qz_b[pb][0], qz_b[pb][1], cm_b], writes=[A_b])

            def emitA_act(i):
                t = tiles[i]
                N, c0 = t["N"], t["c0"]
                S.op("act", lambda a, N=N: a.activation(out=Abank[:, :, 0:N], in_=Abank[:, :, 0:N], func=AF.Exp),
                     writes=[A_b])
                k = t["g"] % 3
                sp = spb[k]
                S.op("act", lambda a, sp=sp, N=N: a.activation(out=sp[:, :, 0:N], in_=Abank[:, :, 0:N], func=AF.Ln,
                                                               bias=onec[:], scale=1.0),
                     reads=[A_b, one_b], writes=[sp_b[k]])
                t["racc"] = None if t["first"] else rstate["cur"]
                if not t["last"]:
                    r = rn[0] % 3
                    rn[0] += 1
                    if t["first"]:
                        if c0 > 0:
                            S.op("dve", lambda v, r=r, c0=c0: v.memset(rab[r][:, :, 0:c0], 0.0), writes=[ra_b[r]])
                        S.op("dve", lambda v, r=r, sp=sp, c0=c0, N=N: v.tensor_copy(
                            out=rab[r][:, :, c0:512], in_=sp[:, :, 0:N]), reads=[sp_b[k]], writes=[ra_b[r]])
                    else:
                        pk = rstate["cur"]
                        if c0 > 0:
                            S.op("dve", lambda v, r=r, pk=pk, c0=c0: v.tensor_copy(
                                out=rab[r][:, :, 0:c0], in_=rab[pk][:, :, 0:c0]), reads=[ra_b[pk]], writes=[ra_b[r]])
                        S.op("dve", lambda v, r=r, pk=pk, sp=sp, c0=c0, N=N: v.tensor_tensor(
                            out=rab[r][:, :, c0:512], in0=rab[pk][:, :, c0:512], in1=sp[:, :, 0:N], op=ALU.add),
                            reads=[sp_b[k], ra_b[pk]], writes=[ra_b[r]])
                    rstate["cur"] = r

            def emitB(i):
                t = tiles[i]
                N, c0 = t["N"], t["c0"]
                bb = t["g"] % 2
                Bk = Bbank[bb]
                k = t["g"] % 3
                sp = spb[k]
                rk = t["racc"]

                def mm(pe, t=t, Bk=Bk, sp=sp, rk=rk, pb=pb, N=N, c0=c0):
                    qk(pe, Bk, t, pb, False)
                    last = None
                    for h in range(2):
                        last = pe.matmul(Bk[:, h, 0:N], ntri, sp[:, h, 0:N], start=False, stop=(rk is None))
                        if rk is not None:
                            last = pe.matmul(Bk[:, h, 0:N], nones, rab[rk][:, h, c0:512], start=False, stop=True)
                    return last

                rd = [kt_b[pb], qz_b[pb][0], qz_b[pb][1], cm_b, sp_b[k]]
                if rk is not None:
                    rd.append(ra_b[rk])
                S.op("pe", mm, reads=rd, writes=[B_b[bb]])

            def emitW(i):
                t = tiles[i]
                N = t["N"]
                bb = t["g"] % 2
                Bk = Bbank[bb]
                k = t["g"] % 3
                ww = wwb[k]
                S.op("act", lambda a, Bk=Bk, ww=ww, N=N: a.activation(out=ww[:, :, 0:N], in_=Bk[:, :, 0:N], func=AF.Exp),
                     reads=[B_b[bb]], writes=[ww_b[k]])

            def emitPV(i):
                t = tiles[i]
                N, c0 = t["N"], t["c0"]
                k = t["g"] % 3
                ww = wwb[k]

                def mm(pe, t=t, ww=ww, pb=pb, N=N, c0=c0):
                    last = None
                    if t["first"]:
                        for h in range(2):
                            pe.matmul(Obank[:, h, :], vv[pb][:, 0, :], zt[:], start=True, stop=False)
                    for h in range(2):
                        last = pe.matmul(Obank[:, h, c0:512], vv[pb][:, t["kb"], :], ww[:, h, 0:N],
                                         start=False, stop=t["last"])
                    return last

                S.op("pe", mm, reads=[vv_b[pb], ww_b[k], one_b], writes=[O_b])
                if t["last"]:
                    qb = t["qb"]
                    ok = qb % 2
                    for h in range(2):
                        S.op("dve", lambda v, h=h, ok=ok: v.tensor_copy(
                            out=ot[ok][h * 64:(h + 1) * 64, :], in_=Obank[h * 64:(h + 1) * 64, h, :]),
                            reads=[O_b], writes=[ot_b[ok]])
                    S.op("sp", lambda q, ok=ok, qb=qb, hp=hp: q.dma_start(
                        out=oT[hp * 128:(hp + 1) * 128, qb * 512:(qb + 1) * 512], in_=ot[ok][:]),
                        reads=[ot_b[ok]], dma=ot_s[ok])

            emitA_pe(0)
            emitA_act(0)
            for i in range(NTL):
                if i + 1 < NTL:
                    emitA_pe(i + 1)
                if i >= 1:
                    emitW(i - 1)
                emitB(i)
                if i + 1 < NTL:
                    emitA_act(i + 1)
                if i >= 1:
                    emitPV(i - 1)
            emitW(NTL - 1)
            emitPV(NTL - 1)
        S.emit()


def phase_oproj(nc, S, src, oT, dst, w_d, consts, NT=512):
    with ExitStack() as es:
        PS = Psum(nc, es)
        sb = lambda name, shape, dt: es.enter_context(nc.sbuf_tensor(_uname(name), shape, dt))
        w = sb("ow", [128, 8, D], BF16)
        ob = [sb("oo%d" % i, [128, 8, NT], BF16) for i in range(2)]
        xr = [sb("oxr%d" % i, [128, NT], F32) for i in range(4)]
        w_b, w_s = Buf(), S.dsem()
        ob_b, ob_s = [Buf(), Buf()], [S.dsem(), S.dsem()]
        xr_b, xr_s = [Buf() for _ in range(4)], [S.dsem() for _ in range(4)]
        ntile = T // NT
        _load_w_cast(S, w[:], w_d.rearrange("(kc p) m -> p kc m", p=128), w_b, w_s)

        def load_o(i):
            b = i % 2
            S.op("sp", lambda q, i=i, b=b: q.dma_start(
                out=ob[b][:], in_=oT[:, i * NT:(i + 1) * NT].rearrange("(c p) t -> p c t", p=128)),
                writes=[ob_b[b]], dma=ob_s[b])

        load_o(0)
        n = 0
        for i in range(ntile):
            b = i % 2
            t0 = i * NT
            if i + 1 < ntile:
                load_o(i + 1)
            for c in range(8):
                k3 = n % 4
                n += 1
                S.op("sp", lambda q, c=c, k3=k3, t0=t0: q.dma_start(
                    out=xr[k3][:], in_=src[c * 128:(c + 1) * 128, t0:t0 + NT]), writes=[xr_b[k3]], dma=xr_s[k3])
                ps, psb = PS.next()

                def mm(pe, c=c, ps=ps, b=b):
                    last = None
                    for k in range(8):
                        last = pe.matmul(ps[:, 0:NT], w[:, k, c * 128:(c + 1) * 128], ob[b][:, k, :],
                                         start=(k == 0), stop=(k == 7))
                    return last

                S.op("pe", mm, reads=[ob_b[b], w_b], writes=[psb])
                S.op("dve", lambda v, ps=ps, k3=k3: v.tensor_tensor(
                    out=xr[k3][:], in0=ps[:, 0:NT], in1=xr[k3][:], op=ALU.add), reads=[psb], writes=[xr_b[k3]])
                S.op("sp", lambda q, c=c, k3=k3, t0=t0: q.dma_start(
                    out=dst[c * 128:(c + 1) * 128, t0:t0 + NT], in_=xr[k3][:]), reads=[xr_b[k3]], dma=xr_s[k3])
        S.emit()


def _consts(nc, S, es, vecs_d, ones_d):
    sb = lambda name, shape, dt: es.enter_context(nc.sbuf_tensor(_uname(name), shape, dt))
    ones = sb("ones_sb", [128, 128], BF16)
    vecs = sb("vecs_sb", [128, NV], F32)
    epsc = sb("epsc", [128, 1], F32)
    cbuf = Buf()
    ds = S.dsem()
    S.op("sp", lambda q: [q.dma_start(out=ones[:], in_=ones_d), q.dma_start(out=vecs[:], in_=vecs_d)],
         writes=[cbuf], dma=ds, ndma=2)
    S.op("dve", lambda v: v.memset(epsc[:], EPS), writes=[cbuf])
    return ones[:], vecs, epsc[:], cbuf


def build(phases, fused):
    nc = bass.Bass("TRN2", target_bir_lowering=False)
    es = ExitStack()
    names_in, names_out = [], []

    cache = {}

    def dram(name, shape, dt, produced_by, consumed_by=None):
        if name not in cache:
            cache[name] = _dram(name, shape, dt, produced_by)
        return cache[name]

    def _dram(name, shape, dt, produced_by):
        if produced_by is None or produced_by not in phases:
            kind = "ExternalInput"
            names_in.append(name)
        elif name == "yT" or (not fused):
            kind = "ExternalOutput"
            names_out.append(name)
        else:
            kind = "Internal"
        return nc.dram_tensor(name, shape, dt, kind=kind).ap()

    vecs_d = dram("vecs", [128, NV], F32, None)
    ones_d = dram("ones", [128, 128], BF16, None)
    S = Sched(nc, es)
    consts = _consts(nc, S, es, vecs_d, ones_d)
    if "lru" in phases:
        src = dram("xT", [D, T], F32, None)
        dst = dram("h1", [D, T], F32, "lru")
        w_in = dram("lru_w_in", [D, 2 * D], F32, None)
        gw_d = dram("lru_gate_w", [2, 4, 256, 256], F32, None)
        w_out = dram("lru_w_out", [D, D], F32, None)
        phase_lru(nc, S, src, dst, w_in, gw_d, w_out, consts)
    if "ffn0" in phases:
        src = dram("h1", [D, T], F32, "lru")
        dst = dram("h2", [D, T], F32, "ffn0")
        w_in = dram("ffn_w_in0", [D, 2 * DFF], F32, None)
        w_out = dram("ffn_w_out0", [DFF, D], F32, None)
        phase_ffn(nc, S, src, dst, w_in, w_out, consts, V_FFN0)
    if "attn" in phases:
        src = dram("h2", [D, T], F32, "ffn0")
        cm_d = dram("cmask", [128, 3 * 128 + 4 * 512], BF16, None)
        qT = dram("qT", [D, T], BF16, "attn")
        kT = dram("kT", [D, T], BF16, "attn")
        vd = dram("vd", [128, 32, D], BF16, "attn")
        oT = dram("oT", [D, T], BF16, "attn")
        dst = dram("h3", [D, T], F32, "attn")
        w_qkv = dram("attn_w_qkv", [D, 3 * D], F32, None)
        w_o = dram("attn_w_o", [D, D], F32, None)
        phase_qkv(nc, S, src, qT, kT, vd, w_qkv, consts)
        phase_attn(nc, S, qT, kT, vd, oT, cm_d, consts)
        phase_oproj(nc, S, src, oT, dst, w_o, consts)
    if "ffn1" in phases:
        src = dram("h3", [D, T], F32, "attn")
        dst = dram("yT", [D, T], F32, "ffn1")
        w_in = dram("ffn_w_in1", [D, 2 * DFF], F32, None)
        w_out = dram("ffn_w_out1", [DFF, D], F32, None)
        phase_ffn(nc, S, src, None, w_in, w_out, consts, V_FFN1, fin_out=dst, fcol=V_FIN)
    es.close()
    return nc, names_in, names_out


def pack_vecs(inp):
    def cols(v):
        v = np.asarray(v, dtype=np.float32).reshape(-1, 128)
        return v.T
    parts = [cols(inp["mix_norm"][0]), cols(inp["ffn_norm"][0]), cols(inp["mix_norm"][1]),
             cols(inp["ffn_norm"][1]), cols(inp["final_norm"]),
             cols(inp["lru_b_in"][0]), cols(inp["lru_conv_w"][0]), cols(inp["lru_conv_b"][0]),
             cols(inp["lru_gate_b"][0]), cols(inp["lru_lambda"][0]), cols(inp["lru_b_out"][0])]
    out = np.ascontiguousarray(np.concatenate(parts, axis=1))
    assert out.shape == (128, NV), out.shape
    return out


def make_cmask():
    j = np.arange(128)[:, None]
    sidx = np.arange(128)[None, :]
    ident = (j == sidx).astype(np.float32)
    ntri = -(j >= sidx).astype(np.float32)
    nones = -np.ones((128, 128), np.float32)
    f = np.arange(512)[None, :]
    p = np.arange(128)[:, None]
    masks = [np.where(f - p > 128 * d, 0.0, NEG).astype(np.float32) for d in range(4)]
    return np.ascontiguousarray(np.concatenate([ident, ntri, nones] + masks, axis=1)).astype(ml_dtypes.bfloat16)


LAUNCHES = [["lru", "ffn0", "attn", "ffn1"]]


def kernel(**inputs):
    inp = {k: np.asarray(v) for k, v in inputs.items()}
    x = inp["x"].astype(np.float32, copy=False)
    n = x.shape[0]
    shared = {
        "vecs": pack_vecs(inp),
        "ones": np.ones((128, 128), dtype=ml_dtypes.bfloat16),
        "cmask": make_cmask(),
        "lru_w_in": np.ascontiguousarray(inp["lru_w_in"][0], dtype=np.float32),
        "lru_gate_w": np.ascontiguousarray(inp["lru_gate_w"][0], dtype=np.float32),
        "lru_w_out": np.ascontiguousarray(inp["lru_w_out"][0], dtype=np.float32),
        "attn_w_qkv": np.ascontiguousarray(inp["attn_w_qkv"][0], dtype=np.float32),
        "attn_w_o": np.ascontiguousarray(inp["attn_w_o"][0], dtype=np.float32),
        "ffn_w_in0": np.ascontiguousarray(inp["ffn_w_in"][0], dtype=np.float32),
        "ffn_w_in1": np.ascontiguousarray(inp["ffn_w_in"][1], dtype=np.float32),
        "ffn_w_out0": np.ascontiguousarray(inp["ffn_w_out"][0], dtype=np.float32),
        "ffn_w_out1": np.ascontiguousarray(inp["ffn_w_out"][1], dtype=np.float32),
    }
    per_core = [{"xT": np.ascontiguousarray(x[c].T)} for c in range(n)]
    fused = len(LAUNCHES) == 1
    for phases in LAUNCHES:
        nc, nin, nout = build(phases, fused)
        in_maps = []
        for c in range(n):
            m = {}
            for name in nin:
                m[name] = shared[name] if name in shared else per_core[c][name]
            in_maps.append(m)
        res = run_bass_kernel_spmd(nc, in_maps, core_ids=list(range(n)))
        for c in range(n):
            for name in nout:
                per_core[c][name] = res.results[c][name]
    out = np.stack([np.ascontiguousarray(per_core[c]["yT"].T) for c in range(n)], axis=0)
    return out.astype(np.float32, copy=False)
```

```python
import numpy as np
import ml_dtypes
from contextlib import ExitStack
import concourse.bass as bass
import concourse.mybir as mybir
from concourse.bass_utils import run_bass_kernel_spmd

F32 = mybir.dt.float32
BF16 = mybir.dt.bfloat16
AF = mybir.ActivationFunctionType
ALU = mybir.AluOpType

T = 4096
D = 1024
DFF = 2816
NJ = DFF // 128
EPS = 1e-6
NEG = -30000.0

V_MIX0, V_FFN0, V_MIX1, V_FFN1, V_FIN = 0, 8, 16, 24, 32
V_BIN = 40
V_CW = 56
V_CB = 88
V_GB = 96
V_LAM = 112
V_BO = 120
NV = 128


_UID = [0]
DBG_NHP = 8
DBG_NQB = 8


def _uname(n):
    _UID[0] += 1
    return "%s_%d" % (n, _UID[0])


class Buf:
    __slots__ = ("w", "r")

    def __init__(self):
        self.w = None
        self.r = {}


class DSem:
    def __init__(self, key):
        self.key = key
        self.val = 0


class Sched:
    ENG = ("pe", "act", "dve", "pool", "sp")
    BLK = {"pe": "tensor", "act": "scalar", "dve": "vector", "pool": "gpsimd", "sp": "sync"}

    def __init__(self, nc, es):
        self.nc = nc
        self.es = es
        self.sems = {}
        self.cnt = {}
        for e in ("pe", "act", "dve", "pool"):
            self.sems[e] = es.enter_context(nc.semaphore("s_" + e))
            self.cnt[e] = 0
        self.ops = {e: [] for e in self.ENG}
        self.waited = {e: {} for e in self.ENG}
        self.ndma = 0
        self.bar = {e: {} for e in self.ENG}

    def dsem(self):
        key = "d%d" % self.ndma
        self.ndma += 1
        self.sems[key] = self.es.enter_context(self.nc.semaphore("s_" + key))
        return DSem(key)

    def op(self, eng, fn, reads=(), writes=(), dma=None, ndma=1):
        deps = dict(self.bar[eng])
        self.bar[eng] = {}

        def add(ev):
            if ev is None:
                return
            k, v = ev
            if deps.get(k, 0) < v:
                deps[k] = v

        for b in reads:
            add(b.w)
        for b in writes:
            add(b.w)
            for k, v in b.r.items():
                add((k, v))
        if dma is not None:
            dma.val += 16 * ndma
            ev = (dma.key, dma.val)
        else:
            self.cnt[eng] += 1
            ev = (eng, self.cnt[eng])
        self.ops[eng].append((fn, deps, ev, dma is not None))
        for b in writes:
            b.w = ev
            b.r = {}
        for b in reads:
            if b.r.get(ev[0], 0) < ev[1]:
                b.r[ev[0]] = ev[1]
        return ev

    def barrier(self):
        allv = {}
        for e in ("pe", "act", "dve", "pool"):
            if self.cnt[e]:
                allv[e] = self.cnt[e]
        for k in self.sems:
            if k.startswith("d"):
                pass
        for k, v in self._dvals().items():
            allv[k] = v
        for e in self.ENG:
            self.bar[e] = dict(allv)

    def _dvals(self):
        out = {}
        for e in self.ENG:
            for (_, _, ev, isd) in self.ops[e]:
                if isd and out.get(ev[0], 0) < ev[1]:
                    out[ev[0]] = ev[1]
        for k, v in getattr(self, "_dv_prev", {}).items():
            if out.get(k, 0) < v:
                out[k] = v
        return out

    def emit(self, final_waits=True):
        dv = self._dvals()
        self._dv_prev = dv
        with self.nc.Block() as block:
            for e in self.ENG:
                ops = self.ops[e]

                def body(eng, ops=ops, e=e):
                    wt = self.waited[e]
                    for (fn, deps, ev, isd) in ops:
                        for k, v in deps.items():
                            if wt.get(k, 0) < v:
                                eng.wait_ge(self.sems[k], v)
                                wt[k] = v
                        insts = fn(eng)
                        if not isinstance(insts, (list, tuple)):
                            insts = [insts]
                        if isd:
                            for i in insts:
                                i.then_inc(self.sems[ev[0]], 16)
                        else:
                            insts[-1].then_inc(self.sems[ev[0]], 1)
                    if e == "sp" and final_waits:
                        for k, v in dv.items():
                            if wt.get(k, 0) < v:
                                eng.wait_ge(self.sems[k], v)
                                wt[k] = v

                if ops or e == "sp":
                    getattr(block, self.BLK[e])(body)
        self.ops = {e: [] for e in self.ENG}


class Psum:
    def __init__(self, nc, es):
        self.t = es.enter_context(nc.psum_tensor(_uname("ps"), [128, 8, 512], F32))
        self.bufs = [Buf() for _ in range(8)]
        self.i = 0

    def next(self, lo=0, hi=8):
        n = hi - lo
        b = lo + (self.i % n)
        self.i += 1
        return self.t[:, b, :], self.bufs[b]


def _rms_stats(S, PS, x_ap, x_bufs, sq, sq_buf, ones, cbuf, epscol, rs, rs_buf, rstd, rstd_buf, NT):
    S.op("act", lambda a: a.activation(out=sq, in_=x_ap, func=AF.Square), reads=x_bufs, writes=[sq_buf])
    ps, psb = PS.next()

    def mm(pe):
        last = None
        for c in range(8):
            last = pe.matmul(ps[:, 0:NT], ones, sq[:, c, :], start=(c == 0), stop=(c == 7))
        return last

    S.op("pe", mm, reads=[sq_buf, cbuf], writes=[psb])
    S.op("act", lambda a: a.activation(out=rs, in_=ps[:, 0:NT], func=AF.Ln, bias=epscol, scale=1.0 / D),
         reads=[psb, cbuf], writes=[rs_buf])
    S.op("act", lambda a: a.activation(out=rstd, in_=rs, func=AF.Exp, scale=-0.5),
         reads=[rs_buf], writes=[rstd_buf])


def _load_w_cast(S, dst_ap, src_ap, wbuf, dsem, nsplit=1):
    k = dst_ap.shape[1]
    parts = list(range(k))

    def fn(g):
        return [g.dma_start(out=dst_ap[:, a, :], in_=src_ap[:, a, :], max_dma_last_dim=4096) for a in parts]

    S.op("pool", fn, writes=[wbuf], dma=dsem, ndma=len(parts))


def phase_ffn(nc, S, src, dst, w_in_d, w_out_d, consts, gcol, fin_out=None, fcol=None, NT=256):
    ones, vecs, epscol, cbuf = consts
    with ExitStack() as es:
        PS = Psum(nc, es)
        sb = lambda name, shape, dt: es.enter_context(nc.sbuf_tensor(_uname(name), shape, dt))
        w1 = sb("w1", [128, 8, 2 * DFF], BF16)
        w2 = sb("w2", [128, NJ, D], BF16)
        xt = [sb("xt%d" % i, [128, 8, NT], F32) for i in range(2)]
        hn = [sb("hn%d" % i, [128, 8, NT], BF16) for i in range(2)]
        sq = sb("sq", [128, 8, NT], BF16)
        hm = sb("hm", [128, NJ, NT], BF16)
        sg = [sb("sg%d" % i, [128, NT], F32) for i in range(3)]
        rs = sb("rs", [128, NT], F32)
        rstd = sb("rstd", [128, NT], F32)
        if fin_out is not None:
            yo = [sb("yo%d" % i, [128, 8, NT], F32) for i in range(1)]
            yo_b = [Buf()]
            yo_s = [S.dsem()]
        w1b = [Buf() for _ in range(4)]
        w2b = [Buf() for _ in range(2)]
        xt_b = [Buf(), Buf()]
        xt_s = [S.dsem(), S.dsem()]
        st_s = [S.dsem(), S.dsem()]
        hn_b = [Buf(), Buf()]
        sq_b, hm_b = Buf(), [Buf() for _ in range(NJ)]
        sg_b = [Buf() for _ in range(3)]
        rs_b, rstd_b = Buf(), Buf()
        wsem = [S.dsem() for _ in range(6)]

        ntile = T // NT

        def load_x(i):
            b = i % 2
            S.op("sp", lambda q, i=i, b=b: q.dma_start(
                out=xt[b][:], in_=src[:, i * NT:(i + 1) * NT].rearrange("(c p) t -> p c t", p=128)),
                writes=[xt_b[b]], dma=xt_s[b])

        load_x(0)
        w_in_v = w_in_d.rearrange("(kc p) m -> p kc m", p=128)
        for qd in range(4):
            _load_w_cast(S, w1[:, 2 * qd:2 * qd + 2, :], w_in_v[:, 2 * qd:2 * qd + 2, :], w1b[qd], wsem[qd], nsplit=2)
        w_out_v = w_out_d.rearrange("(kc p) m -> p kc m", p=128)
        for hf in range(2):
            _load_w_cast(S, w2[:, 11 * hf:11 * hf + 11, :], w_out_v[:, 11 * hf:11 * hf + 11, :], w2b[hf], wsem[4 + hf], nsplit=1)

        def norm(i):
            b = i % 2
            x = xt[b]
            _rms_stats(S, PS, x[:], [xt_b[b]], sq[:], sq_b, ones, cbuf, epscol, rs[:], rs_b, rstd[:], rstd_b, NT)
            for c in range(8):
                S.op("dve", lambda v, c=c, x=x, b=b: v.scalar_tensor_tensor(
                    out=hn[b][:, c, :], in0=x[:, c, :], scalar=vecs[:, gcol + c:gcol + c + 1], in1=rstd[:],
                    op0=ALU.mult, op1=ALU.mult), reads=[xt_b[b], rstd_b, cbuf], writes=[hn_b[b]])

        norm(0)
        for i in range(ntile):
            b = i % 2
            x = xt[b]
            if i + 1 < ntile:
                load_x(i + 1)
            for j in range(NJ):
                pg, pgb = PS.next()
                pu, pub = PS.next()

                def mm1(pe, j=j, pg=pg, pu=pu, b=b):
                    last = None
                    for c in range(8):
                        last = pe.matmul(pg[:, 0:NT], w1[:, c, j * 128:(j + 1) * 128], hn[b][:, c, :],
                                         start=(c == 0), stop=(c == 7))
                    for c in range(8):
                        last = pe.matmul(pu[:, 0:NT], w1[:, c, DFF + j * 128:DFF + (j + 1) * 128], hn[b][:, c, :],
                                         start=(c == 0), stop=(c == 7))
                    return last

                S.op("pe", mm1, reads=[hn_b[b]] + w1b, writes=[pgb, pub])
                k = j % 3
                S.op("act", lambda a, pg=pg, k=k: a.activation(out=sg[k][:], in_=pg[:, 0:NT], func=AF.Silu),
                     reads=[pgb], writes=[sg_b[k]])
                S.op("dve", lambda v, pu=pu, k=k, j=j: v.tensor_tensor(
                    out=hm[:, j, :], in0=pu[:, 0:NT], in1=sg[k][:], op=ALU.mult),
                    reads=[pub, sg_b[k]], writes=[hm_b[j]])
            if i + 1 < ntile:
                norm(i + 1)
            for c in range(8):
                po, pob = PS.next()

                def mm2(pe, c=c, po=po):
                    last = None
                    for j in range(NJ):
                        last = pe.matmul(po[:, 0:NT], w2[:, j, c * 128:(c + 1) * 128], hm[:, j, :],
                                         start=(j == 0), stop=(j == NJ - 1))
                    return last

                S.op("pe", mm2, reads=hm_b + w2b, writes=[pob])
                S.op("dve", lambda v, c=c, po=po, x=x: v.tensor_tensor(
                    out=x[:, c, :], in0=po[:, 0:NT], in1=x[:, c, :], op=ALU.add),
                    reads=[pob], writes=[xt_b[b]])
            if fin_out is None:
                S.op("sp", lambda q, i=i, x=x: q.dma_start(
                    out=dst[:, i * NT:(i + 1) * NT].rearrange("(c p) t -> p c t", p=128), in_=x[:]),
                    reads=[xt_b[b]], dma=st_s[b])
            else:
                _rms_stats(S, PS, x[:], [xt_b[b]], sq[:], sq_b, ones, cbuf, epscol, rs[:], rs_b, rstd[:], rstd_b, NT)
                for c in range(8):
                    S.op("dve", lambda v, c=c, x=x: v.scalar_tensor_tensor(
                        out=yo[0][:, c, :], in0=x[:, c, :], scalar=vecs[:, fcol + c:fcol + c + 1], in1=rstd[:],
                        op0=ALU.mult, op1=ALU.mult), reads=[xt_b[b], rstd_b, cbuf], writes=[yo_b[0]])
                S.op("sp", lambda q, i=i: q.dma_start(
                    out=fin_out[:, i * NT:(i + 1) * NT].rearrange("(c p) t -> p c t", p=128), in_=yo[0][:]),
                    reads=[yo_b[0]], dma=yo_s[0])
        S.emit()


def phase_lru(nc, S, src, dst, w_in_d, gw_d, w_out_d, consts, NT=512):
    ones, vecs, epscol, cbuf = consts
    with ExitStack() as es:
        PS = Psum(nc, es)
        sb = lambda name, shape, dt: es.enter_context(nc.sbuf_tensor(_uname(name), shape, dt))
        w1 = sb("lw1", [128, 8, 2048], BF16)
        gw = sb("lgw", [128, 16, 256], BF16)
        w2 = sb("lw2", [128, 8, D], BF16)
        xt = sb("lxt", [128, 8, NT], F32)
        xr = [sb("lxr%d" % i, [128, NT], F32) for i in range(3)]
        hn = sb("lhn", [128, 8, NT], BF16)
        sq = sb("lsq", [128, 8, NT], BF16)
        rec = [sb("lrec%d" % i, [128, NT + 3], F32) for i in range(3)]
        halo = sb("lhalo", [128, 8, 3], F32)
        hst = sb("lhst", [128, 8], F32)
        rc = sb("lrc", [128, 8, NT], F32)
        rcb = sb("lrcb", [128, 8, NT], BF16)
        ug = sb("lug", [128, 8, NT], F32)
        thr = sb("lthr", [128, 8, NT], F32)
        thi = sb("lthi", [128, 8, NT], F32)
        ta = [sb("lta%d" % i, [128, NT], F32) for i in range(2)]
        ta2 = [sb("lta2%d" % i, [128, NT], F32) for i in range(2)]
        tb = [sb("ltb%d" % i, [128, NT], F32) for i in range(2)]
        thh = [sb("lthh%d" % i, [128, NT], F32) for i in range(2)]
        y = sb("ly", [128, 8, NT], BF16)
        rs = sb("lrs", [128, NT], F32)
        rstd = sb("lrstd", [128, NT], F32)
        pv = sb("lpv", [128, 64], F32)
        w1b, gwb, w2b = [Buf(), Buf()], Buf(), Buf()
        xt_b, xt_s = Buf(), S.dsem()
        xr_b = [Buf() for _ in range(3)]
        xr_s = [S.dsem() for _ in range(3)]
        hn_b, sq_b, rs_b, rstd_b = Buf(), Buf(), Buf(), Buf()
        rec_b = [Buf() for _ in range(3)]
        halo_b = [Buf() for _ in range(8)]
        hst_b = [Buf() for _ in range(8)]
        rc_b = [Buf() for _ in range(8)]
        rcb_b = [Buf() for _ in range(8)]
        ug_b = [Buf() for _ in range(8)]
        thr_b = [Buf() for _ in range(8)]
        thi_b = [Buf() for _ in range(8)]
        ta_b = [Buf(), Buf()]
        ta2_b = [Buf(), Buf()]
        tb_b = [Buf(), Buf()]
        thh_b = [Buf(), Buf()]
        y_b = [Buf() for _ in range(8)]
        pv_b = Buf()
        wsem = [S.dsem() for _ in range(4)]
        ntile = T // NT

        def load_x(i):
            S.op("sp", lambda q, i=i: q.dma_start(
                out=xt[:], in_=src[:, i * NT:(i + 1) * NT].rearrange("(c p) t -> p c t", p=128)),
                writes=[xt_b], dma=xt_s)

        load_x(0)
        w_in_v = w_in_d.rearrange("(kc p) m -> p kc m", p=128)
        for hf in range(2):
            _load_w_cast(S, w1[:, 4 * hf:4 * hf + 4, :], w_in_v[:, 4 * hf:4 * hf + 4, :], w1b[hf], wsem[hf], nsplit=2)
        _load_w_cast(S, gw[:], gw_d.rearrange("g n (kc p) d -> p (g n kc) d", p=128), gwb, wsem[2])
        _load_w_cast(S, w2[:], w_out_d.rearrange("(kc p) m -> p kc m", p=128), w2b, wsem[3], nsplit=2)

        S.op("dve", lambda v: v.memset(pv[:, 40:41], 1.0), writes=[pv_b])
        S.op("dve", lambda v: v.memset(halo[:], 0.0), writes=halo_b)
        S.op("dve", lambda v: v.memset(hst[:], 0.0), writes=hst_b)
        S.op("dve", lambda v: v.tensor_scalar(out=pv[:, 0:16], in0=vecs[:, V_GB:V_GB + 16], scalar1=0.5, scalar2=None,
                                               op0=ALU.mult), reads=[cbuf], writes=[pv_b])
        S.op("act", lambda a: a.activation(out=pv[:, 32:40], in_=vecs[:, V_LAM:V_LAM + 8], func=AF.Exp, scale=-1.0),
             reads=[cbuf], writes=[pv_b])
        S.op("act", lambda a: a.activation(out=pv[:, 32:40], in_=pv[:, 32:40], func=AF.Ln, bias=pv[:, 40:41], scale=1.0),
             writes=[pv_b])
        S.op("dve", lambda v: v.tensor_scalar(out=pv[:, 16:24], in0=pv[:, 32:40], scalar1=-4.0, scalar2=None,
                                               op0=ALU.mult), writes=[pv_b])
        S.op("dve", lambda v: v.tensor_scalar(out=pv[:, 24:32], in0=pv[:, 32:40], scalar1=-8.0, scalar2=None,
                                               op0=ALU.mult), writes=[pv_b])

        xri = [0]

        def norm(i):
            _rms_stats(S, PS, xt[:], [xt_b], sq[:], sq_b, ones, cbuf, epscol, rs[:], rs_b, rstd[:], rstd_b, NT)
            for c in range(8):
                S.op("dve", lambda v, c=c: v.scalar_tensor_tensor(
                    out=hn[:, c, :], in0=xt[:, c, :], scalar=vecs[:, V_MIX0 + c:V_MIX0 + c + 1], in1=rstd[:],
                    op0=ALU.mult, op1=ALU.mult), reads=[xt_b, rstd_b, cbuf], writes=[hn_b])
            if i + 1 < ntile:
                load_x(i + 1)

        def s1(i, kk):
            if kk < 8:
                c = kk
                ps, psb = PS.next()

                def mm(pe, c=c, ps=ps):
                    last = None
                    for k in range(8):
                        last = pe.matmul(ps[:, 0:NT], w1[:, k, D + c * 128:D + (c + 1) * 128], hn[:, k, :],
                                         start=(k == 0), stop=(k == 7))
                    return last

                S.op("pe", mm, reads=[hn_b] + w1b, writes=[psb])
                rk = (i * 8 + c) % 3
                r_ = rec[rk]
                S.op("act", lambda a, ps=ps, r_=r_, c=c: a.activation(
                    out=r_[:, 3:3 + NT], in_=ps[:, 0:NT], func=AF.Identity,
                    bias=vecs[:, V_BIN + 8 + c:V_BIN + 9 + c], scale=1.0), reads=[psb, cbuf], writes=[rec_b[rk]])
                S.op("pool", lambda g, r_=r_, c=c: g.tensor_copy(out=r_[:, 0:3], in_=halo[:, c, :]),
                     reads=[halo_b[c]], writes=[rec_b[rk]])
                S.op("pool", lambda g, r_=r_, c=c: g.tensor_copy(out=halo[:, c, :], in_=r_[:, NT:NT + 3]),
                     reads=[rec_b[rk]], writes=[halo_b[c]])
                S.op("dve", lambda g, r_=r_, c=c: g.tensor_scalar(
                    out=rc[:, c, :], in0=r_[:, 0:NT], scalar1=vecs[:, V_CW + c:V_CW + c + 1],
                    scalar2=vecs[:, V_CB + c:V_CB + c + 1], op0=ALU.mult, op1=ALU.add),
                    reads=[rec_b[rk], cbuf], writes=[rc_b[c]])
                for k in range(1, 4):
                    S.op("dve", lambda g, r_=r_, c=c, k=k: g.scalar_tensor_tensor(
                        out=rc[:, c, :], in0=r_[:, k:k + NT], scalar=vecs[:, V_CW + 8 * k + c:V_CW + 8 * k + c + 1],
                        in1=rc[:, c, :], op0=ALU.mult, op1=ALU.add), reads=[rec_b[rk], cbuf], writes=[rc_b[c]])
                S.op("pool", lambda g, c=c: g.tensor_copy(out=rcb[:, c, :], in_=rc[:, c, :]),
                     reads=[rc_b[c]], writes=[rcb_b[c]])
            else:
                c = kk - 8
                ps, psb = PS.next()

                def mm(pe, c=c, ps=ps):
                    last = None
                    for k in range(8):
                        last = pe.matmul(ps[:, 0:NT], w1[:, k, c * 128:(c + 1) * 128], hn[:, k, :],
                                         start=(k == 0), stop=(k == 7))
                    return last

                S.op("pe", mm, reads=[hn_b] + w1b, writes=[psb])
                S.op("act", lambda a, ps=ps, c=c: a.activation(
                    out=ug[:, c, :], in_=ps[:, 0:NT], func=AF.Gelu_apprx_tanh,
                    bias=vecs[:, V_BIN + c:V_BIN + c + 1], scale=1.0), reads=[psb, cbuf], writes=[ug_b[c]])
                oc = c
                n, dc = oc // 2, oc % 2
                for g_ in range(2):
                    ps, psb = PS.next()

                    def mm(pe, g_=g_, n=n, dc=dc, ps=ps):
                        last = None
                        for kc in range(2):
                            last = pe.matmul(ps[:, 0:NT], gw[:, (g_ * 4 + n) * 2 + kc, dc * 128:(dc + 1) * 128],
                                             rcb[:, n * 2 + kc, :], start=(kc == 0), stop=(kc == 1))
                        return last

                    S.op("pe", mm, reads=[rcb_b[n * 2], rcb_b[n * 2 + 1], gwb], writes=[psb])
                    dstt = thr if g_ == 0 else thi
                    dstb = thr_b if g_ == 0 else thi_b
                    S.op("act", lambda a, ps=ps, dstt=dstt, oc=oc, g_=g_: a.activation(
                        out=dstt[:, oc, :], in_=ps[:, 0:NT], func=AF.Tanh,
                        bias=pv[:, g_ * 8 + oc:g_ * 8 + oc + 1], scale=0.5), reads=[psb, pv_b], writes=[dstb[oc]])

        def s2(i, kk):
            t0 = i * NT
            if kk < 8:
                c = kk
                k2 = c % 2
                S.op("act", lambda a, c=c, k2=k2: a.activation(out=ta[k2][:], in_=thr[:, c, :], func=AF.Exp,
                                                               bias=pv[:, 16 + c:17 + c], scale=pv[:, 16 + c:17 + c]),
                     reads=[thr_b[c], pv_b], writes=[ta_b[k2]])
                S.op("act", lambda a, c=c, k2=k2: a.activation(out=ta2[k2][:], in_=thr[:, c, :], func=AF.Exp,
                                                               bias=pv[:, 24 + c:25 + c], scale=pv[:, 24 + c:25 + c]),
                     reads=[thr_b[c], pv_b], writes=[ta2_b[k2]])
                S.op("act", lambda a, k2=k2: a.activation(out=ta2[k2][:], in_=ta2[k2][:], func=AF.Ln,
                                                          bias=pv[:, 40:41], scale=-1.0),
                     reads=[pv_b], writes=[ta2_b[k2]])
                S.op("act", lambda a, k2=k2: a.activation(out=ta2[k2][:], in_=ta2[k2][:], func=AF.Exp, scale=0.5),
                     writes=[ta2_b[k2]])
                S.op("dve", lambda v, c=c, k2=k2: v.scalar_tensor_tensor(
                    out=tb[k2][:], in0=thi[:, c, :], scalar=1.0, in1=rc[:, c, :], op0=ALU.add, op1=ALU.mult),
                    reads=[thi_b[c], rc_b[c]], writes=[tb_b[k2]])
                S.op("dve", lambda v, k2=k2: v.scalar_tensor_tensor(
                    out=tb[k2][:], in0=tb[k2][:], scalar=0.5, in1=ta2[k2][:], op0=ALU.mult, op1=ALU.mult),
                    reads=[ta2_b[k2]], writes=[tb_b[k2]])
                S.op("dve", lambda v, c=c, k2=k2: v.tensor_tensor_scan(
                    out=thh[k2][:], data0=ta[k2][:], data1=tb[k2][:], initial=hst[:, c:c + 1],
                    op0=ALU.mult, op1=ALU.add), reads=[ta_b[k2], tb_b[k2], hst_b[c]], writes=[thh_b[k2]])
                S.op("dve", lambda v, c=c, k2=k2: v.tensor_copy(out=hst[:, c:c + 1], in_=thh[k2][:, NT - 1:NT]),
                     reads=[thh_b[k2]], writes=[hst_b[c]])
                S.op("dve", lambda v, c=c, k2=k2: v.tensor_tensor(
                    out=y[:, c, :], in0=ug[:, c, :], in1=thh[k2][:], op=ALU.mult),
                    reads=[ug_b[c], thh_b[k2]], writes=[y_b[c]])
            else:
                c = kk - 8
                k3 = xri[0] % 3
                xri[0] += 1
                S.op("sp", lambda q, c=c, k3=k3, t0=t0: q.dma_start(
                    out=xr[k3][:], in_=src[c * 128:(c + 1) * 128, t0:t0 + NT]), writes=[xr_b[k3]], dma=xr_s[k3])
                ps, psb = PS.next()

                def mm(pe, c=c, ps=ps):
                    last = None
                    for k in range(8):
                        last = pe.matmul(ps[:, 0:NT], w2[:, k, c * 128:(c + 1) * 128], y[:, k, :],
                                         start=(k == 0), stop=(k == 7))
                    return last

                S.op("pe", mm, reads=y_b + [w2b], writes=[psb])
                S.op("dve", lambda v, c=c, ps=ps, k3=k3: v.scalar_tensor_tensor(
                    out=xr[k3][:], in0=ps[:, 0:NT], scalar=vecs[:, V_BO + c:V_BO + c + 1], in1=xr[k3][:],
                    op0=ALU.add, op1=ALU.add), reads=[psb, cbuf], writes=[xr_b[k3]])
                S.op("sp", lambda q, c=c, k3=k3, t0=t0: q.dma_start(
                    out=dst[c * 128:(c + 1) * 128, t0:t0 + NT], in_=xr[k3][:]), reads=[xr_b[k3]], dma=xr_s[k3])

        norm(0)
        for kk in range(16):
            s1(0, kk)
        for i in range(ntile):
            nxt = i + 1 < ntile
            if nxt:
                norm(i + 1)
            for kk in range(8):
                if nxt and kk >= 1:
                    s1(i + 1, kk - 1)
                s2(i, kk)
            if nxt:
                s1(i + 1, 7)
            for kk in range(8, 16):
                s2(i, kk)
                if nxt:
                    s1(i + 1, kk)
        S.emit()


def phase_qkv(nc, S, src, qT, kT, vd, w_d, consts, NT=512):
    ones, vecs, epscol, cbuf = consts
    with ExitStack() as es:
        PS = Psum(nc, es)
        sb = lambda name, shape, dt: es.enter_context(nc.sbuf_tensor(_uname(name), shape, dt))
        w = sb("qw", [128, 8, 3 * D], BF16)
        xt = [sb("qxt%d" % i, [128, 8, NT], F32) for i in range(2)]
        hn = sb("qhn", [128, 8, NT], BF16)
        sq = sb("qsq", [128, 8, NT], BF16)
        qst = [sb("qst%d" % i, [128, 8, NT], BF16) for i in range(2)]
        kst = [sb("kst%d" % i, [128, 8, NT], BF16) for i in range(2)]
        vst = [sb("vst%d" % i, [128, 4, D], BF16) for i in range(2)]
        rs = sb("qrs", [128, NT], F32)
        rstd = sb("qrstd", [128, NT], F32)
        wb = [Buf() for _ in range(3)]
        wsem = [S.dsem() for _ in range(3)]
        xt_b, xt_s = [Buf(), Buf()], [S.dsem(), S.dsem()]
        hn_b, sq_b, rs_b, rstd_b = Buf(), Buf(), Buf(), Buf()
        qst_b, kst_b, vst_b = [Buf(), Buf()], [Buf(), Buf()], [Buf(), Buf()]
        qst_s, kst_s, vst_s = [S.dsem(), S.dsem()], [S.dsem(), S.dsem()], [S.dsem(), S.dsem()]
        ntile = T // NT

        def load_x(i):
            b = i % 2
            S.op("sp", lambda q, i=i, b=b: q.dma_start(
                out=xt[b][:], in_=src[:, i * NT:(i + 1) * NT].rearrange("(c p) t -> p c t", p=128)),
                writes=[xt_b[b]], dma=xt_s[b])

        load_x(0)
        w_v = w_d.rearrange("(kc p) m -> p kc m", p=128)
        for part in range(3):
            def fn(g, part=part):
                return [g.dma_start(out=w[:, k, part * D:(part + 1) * D], in_=w_v[:, k, part * D:(part + 1) * D],
                                    max_dma_last_dim=4096) for k in range(8)]
            S.op("pool", fn, writes=[wb[part]], dma=wsem[part], ndma=8)

        for i in range(ntile):
            b = i % 2
            x = xt[b]
            if i + 1 < ntile:
                load_x(i + 1)
            _rms_stats(S, PS, x[:], [xt_b[b]], sq[:], sq_b, ones, cbuf, epscol, rs[:], rs_b, rstd[:], rstd_b, NT)
            for c in range(8):
                S.op("dve", lambda v, c=c, x=x: v.scalar_tensor_tensor(
                    out=hn[:, c, :], in0=x[:, c, :], scalar=vecs[:, V_MIX1 + c:V_MIX1 + c + 1], in1=rstd[:],
                    op0=ALU.mult, op1=ALU.mult), reads=[xt_b[b], rstd_b, cbuf], writes=[hn_b])
            for part, (stg, stg_b, scale) in enumerate(((qst, qst_b, 0.125), (kst, kst_b, 1.0))):
                for c in range(8):
                    ps, psb = PS.next()

                    def mm(pe, c=c, ps=ps, part=part):
                        last = None
                        for k in range(8):
                            last = pe.matmul(ps[:, 0:NT], w[:, k, part * D + c * 128:part * D + (c + 1) * 128],
                                             hn[:, k, :], start=(k == 0), stop=(k == 7))
                        return last

                    S.op("pe", mm, reads=[hn_b, wb[part]], writes=[psb])
                    if c % 2 == 0:
                        S.op("act", lambda a, ps=ps, stg=stg, c=c, scale=scale, b=b: a.activation(
                            out=stg[b][:, c, :], in_=ps[:, 0:NT], func=AF.Copy, scale=scale),
                            reads=[psb], writes=[stg_b[b]])
                    else:
                        S.op("dve", lambda v, ps=ps, stg=stg, c=c, scale=scale, b=b: v.tensor_scalar(
                            out=stg[b][:, c, :], in0=ps[:, 0:NT], scalar1=scale, scalar2=None, op0=ALU.mult),
                            reads=[psb], writes=[stg_b[b]])
                dd = qT if part == 0 else kT
                S.op("sp", lambda q, dd=dd, stg=stg, i=i, b=b: q.dma_start(
                    out=dd[:, i * NT:(i + 1) * NT].rearrange("(c p) t -> p c t", p=128), in_=stg[b][:]),
                    reads=[stg_b[b]], dma=(qst_s if part == 0 else kst_s)[b])
            for tb in range(4):
                for fh in range(2):
                    ps, psb = PS.next()

                    def mm(pe, tb=tb, fh=fh, ps=ps):
                        last = None
                        for k in range(8):
                            last = pe.matmul(ps[:, :], hn[:, k, tb * 128:(tb + 1) * 128],
                                             w[:, k, 2 * D + fh * 512:2 * D + (fh + 1) * 512],
                                             start=(k == 0), stop=(k == 7))
                        return last

                    S.op("pe", mm, reads=[hn_b, wb[2]], writes=[psb])
                    if fh == 0:
                        S.op("act", lambda a, ps=ps, tb=tb, fh=fh, b=b: a.activation(
                            out=vst[b][:, tb, fh * 512:(fh + 1) * 512], in_=ps[:, :], func=AF.Copy),
                            reads=[psb], writes=[vst_b[b]])
                    else:
                        S.op("dve", lambda v, ps=ps, tb=tb, fh=fh, b=b: v.tensor_copy(
                            out=vst[b][:, tb, fh * 512:(fh + 1) * 512], in_=ps[:, :]),
                            reads=[psb], writes=[vst_b[b]])
            S.op("sp", lambda q, i=i, b=b: q.dma_start(out=vd[:, 4 * i:4 * i + 4, :], in_=vst[b][:]),
                 reads=[vst_b[b]], dma=vst_s[b])
        S.emit()


def phase_attn(nc, S, qT, kT, vd, oT, cm_d, consts):
    ones, vecs, epscol, cbuf = consts
    with ExitStack() as es:
        sb = lambda name, shape, dt: es.enter_context(nc.sbuf_tensor(_uname(name), shape, dt))
        pst = es.enter_context(nc.psum_tensor(_uname("aps"), [128, 8, 512], F32))
        cm = sb("acm", [128, 3 * 128 + 4 * 512], BF16)
        ident = cm[:, 0:128]
        ntri = cm[:, 128:256]
        nones = cm[:, 256:384]
        mask0 = cm[:, 384:384 + 512]
        kt = [sb("akt%d" % i, [128, T], BF16) for i in range(2)]
        qz = [[sb("aqz%d_%d" % (i, h), [128, T], BF16) for h in range(2)] for i in range(2)]
        vv = [sb("avv%d" % i, [128, 32, 128], BF16) for i in range(2)]
        spb = [sb("asp%d" % i, [128, 2, 512], BF16) for i in range(3)]
        wwb = [sb("aww%d" % i, [128, 2, 512], BF16) for i in range(3)]
        rab = [sb("ara%d" % i, [128, 2, 512], BF16) for i in range(3)]
        ot = [sb("aot%d" % i, [128, 512], BF16) for i in range(2)]
        zt = sb("azt", [128, 512], BF16)
        onec = sb("aone", [128, 1], F32)
        cm_b, one_b = Buf(), Buf()
        kt_b, vv_b = [Buf(), Buf()], [Buf(), Buf()]
        qz_b = [[Buf(), Buf()], [Buf(), Buf()]]
        ld_s = [S.dsem(), S.dsem()]
        sp_b = [Buf() for _ in range(3)]
        ww_b = [Buf() for _ in range(3)]
        ra_b = [Buf() for _ in range(3)]
        ot_b, ot_s = [Buf(), Buf()], [S.dsem(), S.dsem()]
        A_b = Buf()
        B_b = [Buf(), Buf()]
        O_b = Buf()
        cs = S.dsem()
        S.op("sp", lambda q: q.dma_start(out=cm[:], in_=cm_d), writes=[cm_b], dma=cs)
        S.op("dve", lambda v: v.memset(onec[:], 1.0), writes=[one_b])
        S.op("dve", lambda v: v.memset(zt[:], 0.0), writes=[one_b])
        for i in range(2):
            S.op("pool", lambda g, i=i: g.memset(qz[i][0][64:128, :], 0.0), writes=[qz_b[i][0]])
            S.op("pool", lambda g, i=i: g.memset(qz[i][1][0:64, :], 0.0), writes=[qz_b[i][1]])

        def load_hp(hp):
            pb = hp % 2

            def fn(q, hp=hp, pb=pb):
                return [q.dma_start(out=kt[pb][:], in_=kT[hp * 128:(hp + 1) * 128, :]),
                        q.dma_start(out=qz[pb][0][0:64, :], in_=qT[hp * 128:hp * 128 + 64, :]),
                        q.dma_start(out=qz[pb][1][64:128, :], in_=qT[hp * 128 + 64:(hp + 1) * 128, :]),
                        q.dma_start(out=vv[pb][:], in_=vd[:, :, hp * 128:(hp + 1) * 128])]

            S.op("sp", fn, writes=[kt_b[pb], qz_b[pb][0], qz_b[pb][1], vv_b[pb]], dma=ld_s[pb], ndma=4)

        Abank = pst[:, 0:2, :]
        Bbank = [pst[:, 2:4, :], pst[:, 4:6, :]]
        Obank = pst[:, 6:8, :]
        load_hp(0)
        gi = [0]
        rn = [0]
        for hp in range(DBG_NHP):
            pb = hp % 2
            if hp + 1 < DBG_NHP:
                load_hp(hp + 1)
            tiles = []
            for qb in range(DBG_NQB):
                kbs = list(range(4 * qb + 3, -1, -1))
                for n, kb in enumerate(kbs):
                    dlt = (kb - 4 * qb) if kb >= 4 * qb else None
                    c0 = 128 * dlt if dlt is not None else 0
                    tiles.append(dict(qb=qb, kb=kb, first=(n == 0), last=(n == len(kbs) - 1), dlt=dlt, c0=c0,
                                      N=512 - c0, g=gi[0]))
                    gi[0] += 1
            NTL = len(tiles)
            rstate = {"cur": None}

            def qk(pe, out3, t, pb, final_stop):
                last = None
                N, c0 = t["N"], t["c0"]
                q0 = t["qb"] * 512 + c0
                for h in range(2):
                    last = pe.matmul(out3[:, h, 0:N], kt[pb][:, t["kb"] * 128:(t["kb"] + 1) * 128],
                                     qz[pb][h][:, q0:q0 + N], start=True,
                                     stop=(final_stop and t["dlt"] is None))
                    if t["dlt"] is not None:
                        last = pe.matmul(out3[:, h, 0:N], ident, mask0[:, 0:N], start=False, stop=final_stop)
                return last

            def emitA_pe(i):
                t = tiles[i]
                S.op("pe", lambda pe, t=t, pb=pb: qk(pe, Abank, t, pb, True),
                     reads=[kt_b[pb], qz_b[pb][0], qz_b[pb][1], cm_b], writes=[A_b])

            def emitA_act(i):
                t = tiles[i]
                N, c0 = t["N"], t["c0"]
                S.op("act", lambda a, N=N: a.activation(out=Abank[:, :, 0:N], in_=Abank[:, :, 0:N], func=AF.Exp),
                     writes=[A_b])
                k = t["g"] % 3
                sp = spb[k]
                S.op("act", lambda a, sp=sp, N=N: a.activation(out=sp[:, :, 0:N], in_=Abank[:, :, 0:N], func=AF.Ln,
                                                               bias=onec[:], scale=1.0),
                     reads=[A_b, one_b], writes=[sp_b[k]])
                t["racc"] = None if t["first"] else rstate["cur"]
                if not t["last"]:
                    r = rn[0] % 3
                    rn[0] += 1
                    if t["first"]:
                        if c0 > 0:
                            S.op("dve", lambda v, r=r, c0=c0: v.memset(rab[r][:, :, 0:c0], 0.0), writes=[ra_b[r]])
                        S.op("dve", lambda v, r=r, sp=sp, c0=c0, N=N: v.tensor_copy(
                            out=rab[r][:, :, c0:512], in_=sp[:, :, 0:N]), reads=[sp_b[k]], writes=[ra_b[r]])
                    else:
                        pk = rstate["cur"]
                        if c0 > 0:
                            S.op("dve", lambda v, r=r, pk=pk, c0=c0: v.tensor_copy(
                                out=rab[r][:, :, 0:c0], in_=rab[pk][:, :, 0:c0]), reads=[ra_b[pk]], writes=[ra_b[r]])
                        S.op("dve", lambda v, r=r, pk=pk, sp=sp, c0=c0, N=N: v.tensor_tensor(
                            out=rab[r][:, :, c0:512], in0=rab[pk][:, :, c0:512], in1=sp[:, :, 0:N], op=ALU.add),
                            reads=[sp_b[k], ra_b[pk]], writes=[ra_b[r]])
                    rstate["cur"] = r

            def emitB(i):
                t = tiles[i]
                N, c0 = t["N"], t["c0"]
                bb = t["g"] % 2
                Bk = Bbank[bb]
                k = t["g"] % 3
                sp = spb[k]
                rk = t["racc"]

                def mm(pe, t=t, Bk=Bk, sp=sp, rk=rk, pb=pb, N=N, c0=c0):
                    qk(pe, Bk, t, pb, False)
                    last = None
                    for h in range(2):
                        last = pe.matmul(Bk[:, h, 0:N], ntri, sp[:, h, 0:N], start=False, stop=(rk is None))
                        if rk is not None:
                            last = pe.matmul(Bk[:, h, 0:N], nones, rab[rk][:, h, c0:512], start=False, stop=True)
                    return last

                rd = [kt_b[pb], qz_b[pb][0], qz_b[pb][1], cm_b, sp_b[k]]
                if rk is not None:
                    rd.append(ra_b[rk])
                S.op("pe", mm, reads=rd, writes=[B_b[bb]])

            def emitW(i):
                t = tiles[i]
                N = t["N"]
                bb = t["g"] % 2
                Bk = Bbank[bb]
                k = t["g"] % 3
                ww = wwb[k]
                S.op("act", lambda a, Bk=Bk, ww=ww, N=N: a.activation(out=ww[:, :, 0:N], in_=Bk[:, :, 0:N], func=AF.Exp),
                     reads=[B_b[bb]], writes=[ww_b[k]])

            def emitPV(i):
                t = tiles[i]
                N, c0 = t["N"], t["c0"]
                k = t["g"] % 3
                ww = wwb[k]

                def mm(pe, t=t, ww=ww, pb=pb, N=N, c0=c0):
                    last = None
                    if t["first"]:
                        for h in range(2):
                            pe.matmul(Obank[:, h, :], vv[pb][:, 0, :], zt[:], start=True, stop=False)
                    for h in range(2):
                        last = pe.matmul(Obank[:, h, c0:512], vv[pb][:, t["kb"], :], ww[:, h, 0:N],
                                         start=False, stop=t["last"])
                    return last

                S.op("pe", mm, reads=[vv_b[pb], ww_b[k], one_b], writes=[O_b])
                if t["last"]:
                    qb = t["qb"]
                    ok = qb % 2
                    for h in range(2):
                        S.op("dve", lambda v, h=h, ok=ok: v.tensor_copy(
                            out=ot[ok][h * 64:(h + 1) * 64, :], in_=Obank[h * 64:(h + 1) * 64, h, :]),
                            reads=[O_b], writes=[ot_b[ok]])
                    S.op("sp", lambda q, ok=ok, qb=qb, hp=hp: q.dma_start(
                        out=oT[hp * 128:(hp + 1) * 128, qb * 512:(qb + 1) * 512], in_=ot[ok][:]),
                        reads=[ot_b[ok]], dma=ot_s[ok])

            emitA_pe(0)
            emitA_act(0)
            for i in range(NTL):
                if i + 1 < NTL:
                    emitA_pe(i + 1)
                if i >= 1:
                    emitW(i - 1)
                emitB(i)
                if i + 1 < NTL:
                    emitA_act(i + 1)
                if i >= 1:
                    emitPV(i - 1)
            emitW(NTL - 1)
            emitPV(NTL - 1)
        S.emit()


def phase_oproj(nc, S, src, oT, dst, w_d, consts, NT=512):
    with ExitStack() as es:
        PS = Psum(nc, es)
        sb = lambda name, shape, dt: es.enter_context(nc.sbuf_tensor(_uname(name), shape, dt))
        w = sb("ow", [128, 8, D], BF16)
        ob = [sb("oo%d" % i, [128, 8, NT], BF16) for i in range(2)]
        xr = [sb("oxr%d" % i, [128, NT], F32) for i in range(4)]
        w_b, w_s = Buf(), S.dsem()
        ob_b, ob_s = [Buf(), Buf()], [S.dsem(), S.dsem()]
        xr_b, xr_s = [Buf() for _ in range(4)], [S.dsem() for _ in range(4)]
        ntile = T // NT
        _load_w_cast(S, w[:], w_d.rearrange("(kc p) m -> p kc m", p=128), w_b, w_s)

        def load_o(i):
            b = i % 2
            S.op("sp", lambda q, i=i, b=b: q.dma_start(
                out=ob[b][:], in_=oT[:, i * NT:(i + 1) * NT].rearrange("(c p) t -> p c t", p=128)),
                writes=[ob_b[b]], dma=ob_s[b])

        load_o(0)
        n = 0
        for i in range(ntile):
            b = i % 2
            t0 = i * NT
            if i + 1 < ntile:
                load_o(i + 1)
            for c in range(8):
                k3 = n % 4
                n += 1
                S.op("sp", lambda q, c=c, k3=k3, t0=t0: q.dma_start(
                    out=xr[k3][:], in_=src[c * 128:(c + 1) * 128, t0:t0 + NT]), writes=[xr_b[k3]], dma=xr_s[k3])
                ps, psb = PS.next()

                def mm(pe, c=c, ps=ps, b=b):
                    last = None
                    for k in range(8):
                        last = pe.matmul(ps[:, 0:NT], w[:, k, c * 128:(c + 1) * 128], ob[b][:, k, :],
                                         start=(k == 0), stop=(k == 7))
                    return last

                S.op("pe", mm, reads=[ob_b[b], w_b], writes=[psb])
                S.op("dve", lambda v, ps=ps, k3=k3: v.tensor_tensor(
                    out=xr[k3][:], in0=ps[:, 0:NT], in1=xr[k3][:], op=ALU.add), reads=[psb], writes=[xr_b[k3]])
                S.op("sp", lambda q, c=c, k3=k3, t0=t0: q.dma_start(
                    out=dst[c * 128:(c + 1) * 128, t0:t0 + NT], in_=xr[k3][:]), reads=[xr_b[k3]], dma=xr_s[k3])
        S.emit()


def _consts(nc, S, es, vecs_d, ones_d):
    sb = lambda name, shape, dt: es.enter_context(nc.sbuf_tensor(_uname(name), shape, dt))
    ones = sb("ones_sb", [128, 128], BF16)
    vecs = sb("vecs_sb", [128, NV], F32)
    epsc = sb("epsc", [128, 1], F32)
    cbuf = Buf()
    ds = S.dsem()
    S.op("sp", lambda q: [q.dma_start(out=ones[:], in_=ones_d), q.dma_start(out=vecs[:], in_=vecs_d)],
         writes=[cbuf], dma=ds, ndma=2)
    S.op("dve", lambda v: v.memset(epsc[:], EPS), writes=[cbuf])
    return ones[:], vecs, epsc[:], cbuf


def build(phases, fused):
    nc = bass.Bass("TRN2", target_bir_lowering=False)
    es = ExitStack()
    names_in, names_out = [], []

    cache = {}

    def dram(name, shape, dt, produced_by, consumed_by=None):
        if name not in cache:
            cache[name] = _dram(name, shape, dt, produced_by)
        return cache[name]

    def _dram(name, shape, dt, produced_by):
        if produced_by is None or produced_by not in phases:
            kind = "ExternalInput"
            names_in.append(name)
        elif name == "yT" or (not fused):
            kind = "ExternalOutput"
            names_out.append(name)
        else:
            kind = "Internal"
        return nc.dram_tensor(name, shape, dt, kind=kind).ap()

    vecs_d = dram("vecs", [128, NV], F32, None)
    ones_d = dram("ones", [128, 128], BF16, None)
    S = Sched(nc, es)
    consts = _consts(nc, S, es, vecs_d, ones_d)
    if "lru" in phases:
        src = dram("xT", [D, T], F32, None)
        dst = dram("h1", [D, T], F32, "lru")
        w_in = dram("lru_w_in", [D, 2 * D], F32, None)
        gw_d = dram("lru_gate_w", [2, 4, 256, 256], F32, None)
        w_out = dram("lru_w_out", [D, D], F32, None)
        phase_lru(nc, S, src, dst, w_in, gw_d, w_out, consts)
    if "ffn0" in phases:
        src = dram("h1", [D, T], F32, "lru")
        dst = dram("h2", [D, T], F32, "ffn0")
        w_in = dram("ffn_w_in0", [D, 2 * DFF], F32, None)
        w_out = dram("ffn_w_out0", [DFF, D], F32, None)
        phase_ffn(nc, S, src, dst, w_in, w_out, consts, V_FFN0)
    if "attn" in phases:
        src = dram("h2", [D, T], F32, "ffn0")
        cm_d = dram("cmask", [128, 3 * 128 + 4 * 512], BF16, None)
        qT = dram("qT", [D, T], BF16, "attn")
        kT = dram("kT", [D, T], BF16, "attn")
        vd = dram("vd", [128, 32, D], BF16, "attn")
        oT = dram("oT", [D, T], BF16, "attn")
        dst = dram("h3", [D, T], F32, "attn")
        w_qkv = dram("attn_w_qkv", [D, 3 * D], F32, None)
        w_o = dram("attn_w_o", [D, D], F32, None)
        phase_qkv(nc, S, src, qT, kT, vd, w_qkv, consts)
        phase_attn(nc, S, qT, kT, vd, oT, cm_d, consts)
        phase_oproj(nc, S, src, oT, dst, w_o, consts)
    if "ffn1" in phases:
        src = dram("h3", [D, T], F32, "attn")
        dst = dram("yT", [D, T], F32, "ffn1")
        w_in = dram("ffn_w_in1", [D, 2 * DFF], F32, None)
        w_out = dram("ffn_w_out1", [DFF, D], F32, None)
        phase_ffn(nc, S, src, None, w_in, w_out, consts, V_FFN1, fin_out=dst, fcol=V_FIN)
    es.close()
    return nc, names_in, names_out


def pack_vecs(inp):
    def cols(v):
        v = np.asarray(v, dtype=np.float32).reshape(-1, 128)
        return v.T
    parts = [cols(inp["mix_norm"][0]), cols(inp["ffn_norm"][0]), cols(inp["mix_norm"][1]),
             cols(inp["ffn_norm"][1]), cols(inp["final_norm"]),
             cols(inp["lru_b_in"][0]), cols(inp["lru_conv_w"][0]), cols(inp["lru_conv_b"][0]),
             cols(inp["lru_gate_b"][0]), cols(inp["lru_lambda"][0]), cols(inp["lru_b_out"][0])]
    out = np.ascontiguousarray(np.concatenate(parts, axis=1))
    assert out.shape == (128, NV), out.shape
    return out


def make_cmask():
    j = np.arange(128)[:, None]
    sidx = np.arange(128)[None, :]
    ident = (j == sidx).astype(np.float32)
    ntri = -(j >= sidx).astype(np.float32)
    nones = -np.ones((128, 128), np.float32)
    f = np.arange(512)[None, :]
    p = np.arange(128)[:, None]
    masks = [np.where(f - p > 128 * d, 0.0, NEG).astype(np.float32) for d in range(4)]
    return np.ascontiguousarray(np.concatenate([ident, ntri, nones] + masks, axis=1)).astype(ml_dtypes.bfloat16)


LAUNCHES = [["lru", "ffn0", "attn", "ffn1"]]


def kernel(**inputs):
    inp = {k: np.asarray(v) for k, v in inputs.items()}
    x = inp["x"].astype(np.float32, copy=False)
    n = x.shape[0]
    shared = {
        "vecs": pack_vecs(inp),
        "ones": np.ones((128, 128), dtype=ml_dtypes.bfloat16),
        "cmask": make_cmask(),
        "lru_w_in": np.ascontiguousarray(inp["lru_w_in"][0], dtype=np.float32),
        "lru_gate_w": np.ascontiguousarray(inp["lru_gate_w"][0], dtype=np.float32),
        "lru_w_out": np.ascontiguousarray(inp["lru_w_out"][0], dtype=np.float32),
        "attn_w_qkv": np.ascontiguousarray(inp["attn_w_qkv"][0], dtype=np.float32),
        "attn_w_o": np.ascontiguousarray(inp["attn_w_o"][0], dtype=np.float32),
        "ffn_w_in0": np.ascontiguousarray(inp["ffn_w_in"][0], dtype=np.float32),
        "ffn_w_in1": np.ascontiguousarray(inp["ffn_w_in"][1], dtype=np.float32),
        "ffn_w_out0": np.ascontiguousarray(inp["ffn_w_out"][0], dtype=np.float32),
        "ffn_w_out1": np.ascontiguousarray(inp["ffn_w_out"][1], dtype=np.float32),
    }
    per_core = [{"xT": np.ascontiguousarray(x[c].T)} for c in range(n)]
    fused = len(LAUNCHES) == 1
    for phases in LAUNCHES:
        nc, nin, nout = build(phases, fused)
        in_maps = []
        for c in range(n):
            m = {}
            for name in nin:
                m[name] = shared[name] if name in shared else per_core[c][name]
            in_maps.append(m)
        res = run_bass_kernel_spmd(nc, in_maps, core_ids=list(range(n)))
        for c in range(n):
            for name in nout:
                per_core[c][name] = res.results[c][name]
    out = np.stack([np.ascontiguousarray(per_core[c]["yT"].T) for c in range(n)], axis=0)
    return out.astype(np.float32, copy=False)
```
